# Optimizing a Trainium2 kernel written in Bass

```python
import jax, jax.numpy as jnp
from jax import lax
import numpy as np

D_MODEL = 1024
BATCH = 32
SEQ = 2048
DEPTH = 4

CHUNK = 64
N_MIXERS = 3
N_LAYERS_A = (DEPTH + 2) // 3
N_LAYERS_B = (DEPTH + 1) // 3
N_LAYERS_C = DEPTH // 3
EPS = 1e-6

D_FF = -(-8 * D_MODEL // (3 * 256)) * 256

SG_BLOCK = 128
SG_WIDTH = 2 * D_MODEL
SG_GROUPS = 8
SG_GROUP_DIM = SG_WIDTH // SG_GROUPS

POOL_WINDOWS = (2, 4, 8, 16)
POOL_WIDTH = D_MODEL
POOL_GROUP_DIM = POOL_WIDTH // len(POOL_WINDOWS)

RET_HEADS = D_MODEL // 256
RET_QK_DIM = D_MODEL // RET_HEADS
RET_V_DIM = 2 * D_MODEL // RET_HEADS
RET_IN_WIDTH = 2 * RET_HEADS * RET_QK_DIM + 2 * RET_HEADS * RET_V_DIM
ROPE_BASE = 10000.0

kernel_name = "hybrid_chunk_causal_gmlp_pool_retention_trunk"


def rmsnorm(x, g):
    xf = x.astype(jnp.float32)
    return xf * lax.rsqrt(jnp.mean(xf * xf, axis=-1, keepdims=True) + EPS) * g.astype(jnp.float32)


def spatial_gating_mixer(h, w_in, v_norm_g, w_s, b_s, w_out):
    B, S, _ = h.shape
    z = jax.nn.gelu(h @ w_in, approximate=False)
    u, v = jnp.split(z, 2, axis=-1)
    vf = v.astype(jnp.float32)
    mu = jnp.mean(vf, axis=-1, keepdims=True)
    var = jnp.mean(jnp.square(vf - mu), axis=-1, keepdims=True)
    v = ((vf - mu) * lax.rsqrt(var + EPS) * v_norm_g.astype(jnp.float32)).astype(h.dtype)
    nb = S // SG_BLOCK
    v = v.reshape(B, nb, SG_BLOCK, SG_GROUPS, SG_GROUP_DIM)
    cpos = jnp.arange(SG_BLOCK) // CHUNK
    mask = cpos[:, None] >= cpos[None, :]
    w = jnp.where(mask[None], w_s, 0)
    mixed = jnp.einsum('gij,bnjgc->bnigc', w, v) + b_s.T[:, :, None]
    gated = u * mixed.reshape(B, S, SG_WIDTH)
    return gated @ w_out


def multiscale_pool_mixer(h, w_in, w_grp, b_grp, scale, w_out):
    B, S, _ = h.shape
    z = h @ w_in
    cs = jnp.cumsum(z.astype(jnp.float32), axis=1)
    t = jnp.arange(S)
    zg = z.reshape(B, S, len(POOL_WINDOWS), POOL_GROUP_DIM).astype(jnp.float32)
    groups = []
    for gi, win in enumerate(POOL_WINDOWS):
        csg = cs[..., gi * POOL_GROUP_DIM:(gi + 1) * POOL_GROUP_DIM]
        prev = jnp.pad(csg, ((0, 0), (win, 0), (0, 0)))[:, :S]
        cnt = jnp.minimum(t + 1, win).astype(jnp.float32)
        groups.append((csg - prev) / cnt[None, :, None] - zg[:, :, gi])
    p = jnp.stack(groups, axis=2).astype(h.dtype)
    y = jnp.einsum('bsgc,gcd->bsgd', p, w_grp) + b_grp
    y = y.reshape(B, S, POOL_WIDTH) * scale
    return y @ w_out


def apply_rope(x, pos):
    d = x.shape[-1]
    inv = ROPE_BASE ** (-jnp.arange(0, d, 2, dtype=jnp.float32) / d)
    ang = pos[:, None] * inv[None, :]
    cos = jnp.cos(ang)[None, :, None, :]
    sin = jnp.sin(ang)[None, :, None, :]
    x1, x2 = x[..., : d // 2], x[..., d // 2:]
    return jnp.concatenate([x1 * cos - x2 * sin, x2 * cos + x1 * sin], axis=-1)


def retention_mixer(h, w_in, w_out):
    B, S, _ = h.shape
    H, dk, dv, C = RET_HEADS, RET_QK_DIM, RET_V_DIM, CHUNK
    N = S // C
    proj = h @ w_in
    qk = H * dk
    q, k, v, g = jnp.split(proj, [qk, 2 * qk, 2 * qk + H * dv], axis=-1)
    pos = jnp.arange(S, dtype=jnp.float32)
    q = apply_rope(q.reshape(B, S, H, dk).astype(jnp.float32), pos)
    k = apply_rope(k.reshape(B, S, H, dk).astype(jnp.float32), pos) * (dk ** -0.5)
    v = v.reshape(B, S, H, dv).astype(jnp.float32)
    qc = q.transpose(0, 2, 1, 3).reshape(B, H, N, C, dk)
    kc = k.transpose(0, 2, 1, 3).reshape(B, H, N, C, dk)
    vc = v.transpose(0, 2, 1, 3).reshape(B, H, N, C, dv)

    log_g = jnp.log(1.0 - jnp.exp2(-5.0 - jnp.arange(H, dtype=jnp.float32)))
    idx = jnp.arange(C, dtype=jnp.float32)
    intra_decay = jnp.exp(log_g[:, None, None] * jnp.abs(idx[:, None] - idx[None, :]))
    q_decay = jnp.exp(log_g[:, None] * (idx + 1.0))[None, :, :, None]
    k_decay = jnp.exp(log_g[:, None] * (C - 1.0 - idx))[None, :, :, None]
    chunk_decay = jnp.exp(log_g * C)[None, :, None, None]

    scores = jnp.einsum('bhncd,bhnmd->bhncm', qc, kc) * intra_decay[None, :, None]
    intra = jnp.einsum('bhncm,bhnme->bhnce', scores, vc)

    def step(state, inp):
        qi, ki, vi = inp
        inter = jnp.einsum('bhcd,bhde->bhce', qi * q_decay, state)
        state = state * chunk_decay + jnp.einsum('bhcd,bhce->bhde', ki * k_decay, vi)
        return state, inter

    xs = (jnp.moveaxis(qc, 2, 0), jnp.moveaxis(kc, 2, 0), jnp.moveaxis(vc, 2, 0))
    _, inter = lax.scan(step, jnp.zeros((B, H, dk, dv), jnp.float32), xs)
    o = (intra + jnp.moveaxis(inter, 0, 2)).reshape(B, H, S, dv)
    o = o * lax.rsqrt(jnp.mean(o * o, axis=-1, keepdims=True) + EPS)
    o = o.transpose(0, 2, 1, 3).reshape(B, S, H * dv).astype(h.dtype)
    return (jax.nn.silu(g) * o) @ w_out


def swiglu(h, w_in, w_out):
    a, b = jnp.split(h @ w_in, 2, axis=-1)
    return (jax.nn.silu(a) * b) @ w_out


def setup_inputs(seed: int = 0) -> dict:
    key = jax.random.key(seed)
    ks = jax.random.split(key, 24)
    f32 = jnp.float32
    D = D_MODEL
    nrm = lambda k, shape, s: jax.random.normal(k, shape, f32) * s
    return {
        "x": nrm(ks[0], (BATCH, SEQ, D), 1.0),
        "c": nrm(ks[1], (BATCH, D), 1.0),
        "norm_mix_g": 1.0 + nrm(ks[2], (DEPTH, D), 0.02),
        "norm_ffn_g": 1.0 + nrm(ks[3], (DEPTH, D), 0.02),
        "w_ada": nrm(ks[4], (DEPTH, D, 6 * D), 0.2 * D ** -0.5),
        "b_ada": nrm(ks[5], (DEPTH, 6 * D), 0.02),
        "w_ffn_in": nrm(ks[6], (DEPTH, D, 2 * D_FF), D ** -0.5),
        "w_ffn_out": nrm(ks[7], (DEPTH, D_FF, D), D_FF ** -0.5),
        "sg_w_in": nrm(ks[8], (N_LAYERS_A, D, 2 * SG_WIDTH), D ** -0.5),
        "sg_v_norm_g": 1.0 + nrm(ks[9], (N_LAYERS_A, SG_WIDTH), 0.02),
        "sg_w_s": nrm(ks[10], (N_LAYERS_A, SG_GROUPS, SG_BLOCK, SG_BLOCK), SG_BLOCK ** -0.5),
        "sg_b_s": 1.0 + nrm(ks[11], (N_LAYERS_A, SG_GROUPS, SG_BLOCK), 0.02),
        "sg_w_out": nrm(ks[12], (N_LAYERS_A, SG_WIDTH, D), SG_WIDTH ** -0.5),
        "pool_w_in": nrm(ks[13], (N_LAYERS_B, D, POOL_WIDTH), D ** -0.5),
        "pool_w_grp": nrm(ks[14], (N_LAYERS_B, len(POOL_WINDOWS), POOL_GROUP_DIM, POOL_GROUP_DIM), POOL_GROUP_DIM ** -0.5),
        "pool_b_grp": nrm(ks[15], (N_LAYERS_B, len(POOL_WINDOWS), POOL_GROUP_DIM), 0.02),
        "pool_scale": 1.0 + nrm(ks[16], (N_LAYERS_B, POOL_WIDTH), 0.02),
        "pool_w_out": nrm(ks[17], (N_LAYERS_B, POOL_WIDTH, D), POOL_WIDTH ** -0.5),
        "ret_w_in": nrm(ks[18], (N_LAYERS_C, D, RET_IN_WIDTH), D ** -0.5),
        "ret_w_out": nrm(ks[19], (N_LAYERS_C, RET_HEADS * RET_V_DIM, D), (RET_HEADS * RET_V_DIM) ** -0.5),
        "final_norm_g": 1.0 + nrm(ks[20], (D,), 0.02),
    }


def reference(x, c, norm_mix_g, norm_ffn_g, w_ada, b_ada, w_ffn_in, w_ffn_out,
              sg_w_in, sg_v_norm_g, sg_w_s, sg_b_s, sg_w_out,
              pool_w_in, pool_w_grp, pool_b_grp, pool_scale, pool_w_out,
              ret_w_in, ret_w_out, final_norm_g):
    dt = x.dtype
    cond = jax.nn.silu(c.astype(jnp.float32))
    for l in range(DEPTH):
        mod = cond @ w_ada[l].astype(jnp.float32) + b_ada[l].astype(jnp.float32)
        sh1, sc1, gt1, sh2, sc2, gt2 = jnp.split(mod[:, None, :], 6, axis=-1)
        h = (rmsnorm(x, norm_mix_g[l]) * (1.0 + sc1) + sh1).astype(dt)
        kind, j = l % N_MIXERS, l // N_MIXERS
        if kind == 0:
            y = spatial_gating_mixer(h, sg_w_in[j], sg_v_norm_g[j], sg_w_s[j], sg_b_s[j], sg_w_out[j])
        elif kind == 1:
            y = multiscale_pool_mixer(h, pool_w_in[j], pool_w_grp[j], pool_b_grp[j], pool_scale[j], pool_w_out[j])
        else:
            y = retention_mixer(h, ret_w_in[j], ret_w_out[j])
        x = (x + (1.0 + gt1) * y).astype(dt)
        h = (rmsnorm(x, norm_ffn_g[l]) * (1.0 + sc2) + sh2).astype(dt)
        x = (x + (1.0 + gt2) * swiglu(h, w_ffn_in[l], w_ffn_out[l])).astype(dt)
    return rmsnorm(x, final_norm_g).astype(dt)
```

```python
import numpy as np
from contextlib import ExitStack
import concourse.bass as bass
import concourse.mybir as mybir
from concourse.bass_utils import run_bass_kernel_spmd

F32 = mybir.dt.float32
BF16 = mybir.dt.bfloat16
AF = mybir.ActivationFunctionType
ALU = mybir.AluOpType

NCORES = 8
D = 1024
SEQ = 2048
BATCH = 32
DEPTH = 4
DFF = 2816
EPS = 1e-6
T = 1024
TG = T // 512
NB = T // 128
UPS = SEQ // T
SLOT = 4096
NSLOT = 6
SCR = 24576

V_GMIX, V_GFFN, V_BADA, V_GFIN, V_SGG, V_PB, V_PS = 0, 32, 64, 256, 264, 296, 304
V_ROWS = 384


class SemC:
    def __init__(self, h):
        self.h = h
        self.count = 0


class Eng:
    def __init__(self, e, sem, is_pe=False):
        self.e = e
        self.sem = sem
        self.is_pe = is_pe
        self.waited = {}


class Buf:
    __slots__ = ("w", "r")

    def __init__(self):
        self.w = None
        self.r = {}


class Gen:
    def __init__(self, nseq, layers, final, do_mixer=True, do_ffn=True):
        self.nseq = nseq
        self.layers = layers
        self.final = final
        self.do_mixer = do_mixer
        self.do_ffn = do_ffn

    def _wait(self, E, reads, writes):
        need = {}
        own = E.sem
        for b in reads:
            if b.w is not None:
                s, v = b.w
                if need.get(s, 0) < v:
                    need[s] = v
        for b in writes:
            if b.w is not None:
                s, v = b.w
                if s is not own and need.get(s, 0) < v:
                    need[s] = v
            for s, v in b.r.items():
                if s is not own and need.get(s, 0) < v:
                    need[s] = v
        for s, v in need.items():
            if E.is_pe and s is own:
                continue
            if E.waited.get(s, 0) >= v:
                continue
            E.e.wait_ge(s.h, v)
            E.waited[s] = v

    def op(self, E, reads, writes, fn):
        self._wait(E, reads, writes)
        ins = fn(E.e)
        E.sem.count += 1
        ins.then_inc(E.sem.h, 1)
        c = E.sem.count
        for b in reads:
            b.r[E.sem] = c
        for b in writes:
            b.w = (E.sem, c)
            b.r = {}

    def dma(self, Q, sem, reads, writes, fn):
        self._wait(Q, reads, writes)
        inss = fn(Q.e)
        for ins in inss:
            sem.count += 16
            ins.then_inc(sem.h, 16)
        c = sem.count
        for b in reads:
            b.r[sem] = c
        for b in writes:
            b.w = (sem, c)
            b.r = {}

    def fenced(self, n):
        snap = {}
        for s in self.all_sems:
            if s.count > 0:
                snap[s] = s.count
        out = []
        for _ in range(n):
            b = Buf()
            b.r = dict(snap)
            out.append(b)
        return out

    def tile(self, name, shape, dt):
        return self.es.enter_context(self.nc.sbuf_tensor(name, shape, dt))

    def sem(self, name):
        s = SemC(self.es.enter_context(self.nc.semaphore(name)))
        self.all_sems.append(s)
        return s

    def flush_pending(self):
        if self.pending is not None:
            args, tg = self.pending
            self.pending = None
            self.norm_mod(*args, tg)

    def htb(self, tg):
        if self.pending is not None and self.pending[1] == tg:
            self.flush_pending()
        return [self.HT_b[k][tg] for k in range(8)]

    def nb(self):
        b = self.bank_i
        self.bank_i = (b + 1) % 8
        return b

    def tf_next(self):
        i = self.tf_i
        self.tf_i = (i + 1) % len(self.TF)
        return self.TF[i], self.TF_b[i]

    def tb_next(self):
        i = self.tb_i
        self.tb_i = (i + 1) % len(self.TB)
        return self.TB[i], self.TB_b[i]

    def scr_bf(self, off, n):
        return self.SCRT[:, off:off + n]

    def scr_f32(self, off, n):
        return self.SCRT[:, off:off + n].bitcast(F32)

    def load_slab(self, pieces):
        slot = self.ring_i
        self.ring_i = (slot + 1) % NSLOT
        rt = self.ring_t[slot]

        def fn(e):
            inss = []
            for (off, nk, ncl, src) in pieces:
                dst = rt[:, off:off + nk * ncl].rearrange("p (k c) -> p k c", k=nk)
                inss.append(e.dma_start(out=dst, in_=src.rearrange("(k p) c -> p k c", p=128)))
            return inss

        self.dma(self.PQ, self.ring_sem[slot], [], [self.ring_b[slot]], fn)
        return slot

    def wv(self, slot, off, nk, ncl):
        return self.ring_t[slot][:, off:off + nk * ncl].rearrange("p (k c) -> p k c", k=nk)

    def build(self):
        nc = bass.Bass("TRN2", target_bir_lowering=False)
        self.nc = nc
        nseq = self.nseq
        NT = nseq * SEQ

        def din(name, shape):
            return nc.dram_tensor(name, list(shape), F32, kind="ExternalInput").ap()

        self.x_d = din("x", (NT, D))
        self.ct_d = din("ct", (8 * nseq, 128))
        self.vec_d = din("vecs", (V_ROWS, 128))
        L = self.layers
        self.w_ada = {l: din(f"w_ada{l}", (D, 6 * D)) for l in L}
        self.w_ffn_in = {l: din(f"w_ffn_in{l}", (D, 2 * DFF)) for l in L if self.do_ffn}
        self.w_ffn_out = {l: din(f"w_ffn_out{l}", (DFF, D)) for l in L if self.do_ffn}
        sgj = sorted({l // 3 for l in L if l % 3 == 0 and self.do_mixer})
        self.sg_w_in = {j: din(f"sg_w_in{j}", (D, 4096)) for j in sgj}
        self.sg_w_out = {j: din(f"sg_w_out{j}", (2048, D)) for j in sgj}
        self.sg_w_s = din("sg_w_s", (2, 8, 128, 128))
        self.sg_b_s = din("sg_b_s", (2, 8, 128))
        if 1 in L and self.do_mixer:
            self.pool_w_in = din("pool_w_in", (D, D))
            self.pool_w_grp = din("pool_w_grp", (4, 256, 256))
            self.pool_w_out = din("pool_w_out", (D, D))
        if 2 in L and self.do_mixer:
            self.ret_w_in = din("ret_w_in", (D, 6144))
            self.ret_w_out = din("ret_w_out", (2048, D))
        self.c_ident = din("c_ident", (128, 128))
        self.c_poolA = din("c_poolA", (12, 128, 128))
        self.c_rope = din("c_rope", (2, 128, SEQ))
        self.c_dmask = din("c_dmask", (4, 128, 128))
        self.c_rvec = din("c_rvec", (128, 8))
        self.c_sgmask = din("c_sgmask", (128, 128))
        self.y_d = nc.dram_tensor("y", [NT, D], F32, kind="ExternalOutput").ap()

        with ExitStack() as es:
            self.es = es
            self.all_sems = []
            self.PE = Eng(nc.tensor, self.sem("s_pe"), is_pe=True)
            self.ACT = Eng(nc.scalar, self.sem("s_act"))
            self.DVE = Eng(nc.vector, self.sem("s_dve"))
            self.SPQ = Eng(nc.sync, None)
            self.PQ = Eng(nc.gpsimd, None)
            self.csem = self.sem("s_const")
            self.xin_sem = [self.sem("s_xin0"), self.sem("s_xin1")]
            self.yout_sem = [self.sem("s_yo0"), self.sem("s_yo1")]
            self.rope_sem = self.sem("s_rope")
            self.ring_sem = [self.sem(f"s_ring{i}") for i in range(NSLOT)]

            self.ps = [es.enter_context(nc.psum_tensor(f"ps{i}", [128, 512], F32)) for i in range(8)]
            self.ps_b = [Buf() for _ in range(8)]
            self.bank_i = 0
            self.pending = None

            self.XT = self.tile("XT", [128, 8, T], F32)
            self.XT_b = [[Buf() for _ in range(TG)] for _ in range(8)]
            self.HT = self.tile("HT", [128, 8, T], BF16)
            self.HT_b = [[Buf() for _ in range(TG)] for _ in range(8)]
            self.ring_t = [self.tile(f"ring{i}", [128, SLOT], BF16) for i in range(NSLOT)]
            self.ring_b = [Buf() for _ in range(NSLOT)]
            self.ring_i = 0
            self.SCRT = self.tile("scr", [128, SCR], BF16)
            self.S32 = self.tile("S32", [128, 8, 512], F32)
            self.SBF = self.tile("SBF", [128, 8, 512], BF16)
            self.S_b = [Buf() for _ in range(8)]
            self.SBF_b = [Buf() for _ in range(8)]
            self.ZC = self.tile("ZC", [128, 1024], BF16)
            self.ZC_b = Buf()
            self.TF = [self.tile(f"tf{i}", [128, 512], F32) for i in range(4)]
            self.TF_b = [Buf() for _ in range(4)]
            self.tf_i = 0
            self.TB = [self.tile(f"tb{i}", [128, 512], BF16) for i in range(3)]
            self.TB_b = [Buf() for _ in range(3)]
            self.tb_i = 0
            self.identf = self.tile("identf", [128, 128], F32)
            self.identb = self.tile("identb", [128, 128], BF16)
            self.onesb = self.tile("onesb", [128, 128], BF16)
            self.poolA = self.tile("poolA", [128, 12, 128], BF16)
            self.dmask = self.tile("dmask", [128, 4, 128], F32)
            self.rvec = self.tile("rvec", [128, 8], F32)
            self.sgmask = self.tile("sgmask", [128, 128], F32)
            self.vecT = self.tile("vecT", [128, V_ROWS], F32)
            self.condT = self.tile("condT", [128, 8 * nseq], BF16)
            self.MOD = self.tile("MOD", [128, DEPTH * 48 * nseq], F32)
            self.WsT = self.tile("WsT", [128, 16, 128], BF16)
            self.Btile = self.tile("Btile", [128, 16, 128], BF16)
            self.PBS = self.tile("PBS", [128, 8], F32)
            self.stat = self.tile("stat", [128, 64], F32)
            self.epsc = self.tile("epsc", [128, 1], F32)
            self.stat_b = Buf()
            self.const_b = Buf()

            self.preamble()
            for s in range(nseq):
                for hf in range(UPS):
                    self.unit(s, hf)
            for i in range(2):
                if self.yout_sem[i].count > 0:
                    nc.sync.wait_ge(self.yout_sem[i].h, self.yout_sem[i].count)
        return nc

    def mod_ap(self, l, j, c, s):
        nseq = self.nseq
        o = ((l * 6 + j) * 8 + c) * nseq + s
        return self.MOD[:, o:o + 1]

    def preamble(self):
        nc = self.nc
        nseq = self.nseq
        PE, ACT, DVE, SPQ, PQ = self.PE, self.ACT, self.DVE, self.SPQ, self.PQ
        cb = self.const_b
        o = 0
        VR = self.scr_f32(o, 2 * 384)
        o += 2 * 384
        CR = self.scr_f32(o, 2 * 128)
        o += 2 * 128
        WS = self.scr_f32(o, 2 * 16 * 128)
        o += 2 * 16 * 128
        sb = Buf()

        def cl(e):
            inss = []
            inss.append(e.dma_start(out=self.identf[:, :], in_=self.c_ident))
            inss.append(e.dma_start(out=self.dmask[:, :, :], in_=self.c_dmask.rearrange("h m c -> m h c")))
            inss.append(e.dma_start(out=self.rvec[:, :], in_=self.c_rvec))
            inss.append(e.dma_start(out=self.sgmask[:, :], in_=self.c_sgmask))
            inss.append(e.dma_start(out=VR.rearrange("p (t f) -> p t f", t=3),
                                    in_=self.vec_d.rearrange("(t p) f -> p t f", p=128)))
            inss.append(e.dma_start(out=CR[0:8 * nseq, :], in_=self.ct_d))
            inss.append(e.dma_start(out=WS.rearrange("p (a j) -> p a j", a=16),
                                    in_=self.sg_w_s.rearrange("l g i j -> i (l g) j")))
            return inss

        self.dma(SPQ, self.csem, [], [cb, sb], cl)

        def cl2(e):
            inss = []
            inss.append(e.dma_start(out=self.identb[:, :], in_=self.c_ident))
            inss.append(e.dma_start(out=self.poolA[:, :, :], in_=self.c_poolA.rearrange("a p t -> p a t")))
            inss.append(e.dma_start(out=self.Btile[:, :, :].rearrange("p a i -> p (a i)"),
                                    in_=self.sg_b_s.rearrange("l g i -> (l g i)").partition_broadcast(128)))
            return inss

        self.dma(PQ, self.csem, [], [cb], cl2)
        self.op(DVE, [], [cb], lambda e: e.memset(self.onesb[:, :], 1.0))
        self.op(DVE, [], [self.stat_b], lambda e: e.memset(self.stat[:, :], 0.0))
        self.op(DVE, [], [cb], lambda e: e.memset(self.epsc[:, :], EPS))

        bk = self.nb()

        def tr(e):
            ins = None
            for t in range(3):
                ins = e.transpose(self.ps[bk][:, t * 128:(t + 1) * 128], VR[:, t * 128:(t + 1) * 128], self.identf[:, :])
            return ins

        self.op(PE, [cb, sb], [self.ps_b[bk]], tr)
        self.op(DVE, [self.ps_b[bk]], [cb], lambda e: e.tensor_copy(out=self.vecT[:, :], in_=self.ps[bk][:, 0:384]))
        bk2 = self.nb()
        self.op(PE, [cb, sb], [self.ps_b[bk2]],
                lambda e: e.transpose(self.ps[bk2][:, 0:8 * nseq], CR[0:8 * nseq, :], self.identf[0:8 * nseq, 0:8 * nseq]))
        self.op(ACT, [self.ps_b[bk2]], [cb],
                lambda e: e.activation(out=self.condT[:, :], in_=self.ps[bk2][:, 0:8 * nseq], func=AF.Silu))
        for a0 in range(0, 16, 4):
            bk3 = self.nb()

            def tr2(e, a0=a0, bk3=bk3):
                ins = None
                for q in range(4):
                    ins = e.transpose(self.ps[bk3][:, q * 128:(q + 1) * 128], WS[:, (a0 + q) * 128:(a0 + q + 1) * 128],
                                      self.identf[:, :])
                return ins

            self.op(PE, [cb, sb], [self.ps_b[bk3]], tr2)
            self.op(DVE, [self.ps_b[bk3], cb], [cb],
                    lambda e, a0=a0, bk3=bk3: e.tensor_tensor(
                        out=self.WsT[:, a0:a0 + 4, :],
                        in0=self.ps[bk3][:, :].rearrange("p (a i) -> p a i", a=4),
                        in1=self.sgmask[:, :].unsqueeze(1).to_broadcast([128, 4, 128]),
                        op=ALU.mult))
        self.op(DVE, [cb], [cb], lambda e: e.tensor_tensor(out=self.PBS[:, :], in0=self.vecT[:, V_PB:V_PB + 8],
                                                          in1=self.vecT[:, V_PS:V_PS + 8], op=ALU.mult))
        self.mod_layer(self.layers[0]) if self.layers else None

    def mod_layer(self, l):
        nseq = self.nseq
        PE, ACT, DVE = self.PE, self.ACT, self.DVE
        cb = self.const_b
        bkm = self.nb()
        first = True
        for sl in range(12):
            slot = self.load_slab([(0, 8, 512, self.w_ada[l][:, sl * 512:(sl + 1) * 512])])
            w = self.wv(slot, 0, 8, 512)

            def mm(e, sl=sl, w=w, bkm=bkm):
                ins = None
                for fc in range(4):
                    fi = sl * 4 + fc
                    for k in range(8):
                        ins = e.matmul(self.ps[bkm][:, fi * nseq:(fi + 1) * nseq], w[:, k, fc * 128:(fc + 1) * 128],
                                       self.condT[:, k * nseq:(k + 1) * nseq], start=(k == 0), stop=(k == 7))
                return ins

            self.op(PE, [self.ring_b[slot], cb], [self.ps_b[bkm]], mm)
        mv = self.MOD[:, l * 48 * nseq:(l + 1) * 48 * nseq].rearrange("p (f s) -> p f s", s=nseq)
        self.op(DVE, [self.ps_b[bkm], cb], [cb],
                lambda e, mv=mv, bkm=bkm, l=l: e.tensor_tensor(
                    out=mv, in0=self.ps[bkm][:, 0:48 * nseq].rearrange("p (f s) -> p f s", s=nseq),
                    in1=self.vecT[:, V_BADA + l * 48:V_BADA + (l + 1) * 48].unsqueeze(2).to_broadcast([128, 48, nseq]),
                    op=ALU.add))
        for (j, gbase) in ((1, V_GMIX), (4, V_GFFN)):
            v = mv[:, j * 8:(j + 1) * 8, :]
            self.op(DVE, [cb], [cb],
                    lambda e, v=v, gbase=gbase, l=l: e.scalar_tensor_tensor(
                        out=v, in0=v, scalar=1.0, op0=ALU.add,
                        in1=self.vecT[:, gbase + l * 8:gbase + (l + 1) * 8].unsqueeze(2).to_broadcast([128, 8, nseq]),
                        op1=ALU.mult))
        for j in (2, 5):
            v = mv[:, j * 8:(j + 1) * 8, :]
            self.op(DVE, [cb], [cb], lambda e, v=v: e.tensor_scalar(out=v, in0=v, scalar1=1.0, scalar2=None, op0=ALU.add))

    def unit(self, s, hf):
        row0 = s * SEQ + hf * T
        subs = []
        for l in self.layers:
            if self.do_mixer:
                subs.append(("mix", l))
            if self.do_ffn:
                subs.append(("ffn", l))
        nargs = [((l, 1, 0, s) if k == "mix" else (l, 4, 3, s)) for (k, l) in subs]
        self.next_norm = nargs[0] if subs else None
        self.load_unit(row0)
        first_unit = (s == 0 and hf == 0)
        for i, (k, l) in enumerate(subs):
            if first_unit and (i == 0 or subs[i - 1][1] != l):
                li = self.layers.index(l)
                if li + 1 < len(self.layers):
                    self.mod_layer(self.layers[li + 1])
            self.next_norm = nargs[i + 1] if i + 1 < len(subs) else None
            if k == "ffn":
                self.ffn(l, s)
            else:
                kind, j = l % 3, l // 3
                if kind == 0:
                    self.sg_mixer(l, j, s)
                elif kind == 1:
                    self.pool_mixer(l, s, hf)
                else:
                    self.ret_mixer(l, s, hf)
        self.flush_pending()
        self.store_unit(row0)

    def load_unit(self, row0):
        PE, ACT, DVE, SPQ = self.PE, self.ACT, self.DVE, self.SPQ
        xin = [self.scr_f32(0, 2048), self.scr_f32(2048, 2048)]
        xb = self.fenced(2)
        for blk in range(NB):
            i = blk % 2
            self.dma(SPQ, self.xin_sem[i], [], [xb[i]],
                     lambda e, i=i, blk=blk: [e.dma_start(out=xin[i], in_=self.x_d[row0 + blk * 128:row0 + (blk + 1) * 128, :])])
            for h in range(2):
                bk = self.nb()

                def tr(e, i=i, h=h, bk=bk):
                    ins = None
                    for q in range(4):
                        c = h * 4 + q
                        ins = e.transpose(self.ps[bk][:, q * 128:(q + 1) * 128], xin[i][:, c * 128:(c + 1) * 128], self.identf[:, :])
                    return ins

                self.op(PE, [xb[i], self.const_b], [self.ps_b[bk]], tr)
                E = ACT if h == 0 else DVE
                wr = [self.XT_b[h * 4 + q][blk // 4] for q in range(4)]
                src = self.ps[bk][:, :].rearrange("p (a t) -> p a t", a=4)
                dst = self.XT[:, h * 4:h * 4 + 4, blk * 128:(blk + 1) * 128]
                if h == 0:
                    self.op(ACT, [self.ps_b[bk]], wr, lambda e, src=src, dst=dst: e.activation(out=dst, in_=src, func=AF.Copy))
                else:
                    self.op(DVE, [self.ps_b[bk]], wr, lambda e, src=src, dst=dst: e.tensor_copy(out=dst, in_=src))
            if blk % 4 == 3 and self.next_norm is not None:
                l_, jG_, jSH_, s_ = self.next_norm
                self.norm_mod(l_, jG_, jSH_, s_, blk // 4)

    def store_unit(self, row0):
        PE, ACT, DVE, SPQ = self.PE, self.ACT, self.DVE, self.SPQ
        YF = self.scr_f32(0, 8192).rearrange("p (c t) -> p c t", c=8)
        yout = [self.scr_f32(8192, 2048), self.scr_f32(10240, 2048)]
        fb = self.fenced(3)
        yfb, yb = fb[0], fb[1:3]
        for tg in range(TG):
            if self.final:
                self.norm_mod(None, None, None, None, tg, final_dst=(YF, yfb))
            for b4 in range(4):
                blk = tg * 4 + b4
                i = blk % 2
                for h in range(2):
                    bk = self.nb()

                    def tr(e, h=h, bk=bk, b4=b4, blk=blk):
                        ins = None
                        for q in range(4):
                            c = h * 4 + q
                            if self.final:
                                src = YF[:, c, b4 * 128:(b4 + 1) * 128]
                            else:
                                src = self.XT[:, c, blk * 128:(blk + 1) * 128]
                            ins = e.transpose(self.ps[bk][:, q * 128:(q + 1) * 128], src, self.identf[:, :])
                        return ins

                    rd = [yfb] if self.final else [self.XT_b[h * 4 + q][tg] for q in range(4)]
                    self.op(PE, rd + [self.const_b], [self.ps_b[bk]], tr)
                    dst = yout[i][:, h * 512:(h + 1) * 512]
                    if h == 0:
                        self.op(ACT, [self.ps_b[bk]], [yb[i]], lambda e, dst=dst, bk=bk: e.activation(out=dst, in_=self.ps[bk][:, :], func=AF.Copy))
                    else:
                        self.op(DVE, [self.ps_b[bk]], [yb[i]], lambda e, dst=dst, bk=bk: e.tensor_copy(out=dst, in_=self.ps[bk][:, :]))
                self.dma(SPQ, self.yout_sem[i], [yb[i]], [],
                         lambda e, i=i, blk=blk: [e.dma_start(out=self.y_d[row0 + blk * 128:row0 + (blk + 1) * 128, :], in_=yout[i])])

    def norm_mod(self, l, jG, jSH, s, tg, final_dst=None):
        PE, ACT, DVE = self.PE, self.ACT, self.DVE
        sl = slice(tg * 512, (tg + 1) * 512)
        bk = self.nb()
        for c in range(8):
            tb, tbb = self.tb_next()
            self.op(ACT, [self.XT_b[c][tg]], [tbb], lambda e, tb=tb, c=c: e.activation(out=tb[:, :], in_=self.XT[:, c, sl], func=AF.Square))
            self.op(PE, [tbb, self.const_b], [self.ps_b[bk]],
                    lambda e, tb=tb, c=c: e.matmul(self.ps[bk][:, :], self.onesb[:, :], tb[:, :], start=(c == 0), stop=(c == 7)))
        rt, rtb = self.tf_next()
        self.op(ACT, [self.ps_b[bk]], [rtb], lambda e: e.activation(out=rt[:, :], in_=self.ps[bk][:, :], func=AF.Sqrt, scale=1.0 / D, bias=self.epsc[:, 0:1]))
        self.op(DVE, [rtb], [self.ps_b[bk]], lambda e: e.reciprocal(out=self.ps[bk][:, :], in_=rt[:, :]))
        for c in range(8):
            tf, tfb = self.tf_next()
            self.op(DVE, [self.XT_b[c][tg], self.ps_b[bk]], [tfb],
                    lambda e, tf=tf, c=c: e.tensor_tensor(out=tf[:, :], in0=self.XT[:, c, sl], in1=self.ps[bk][:, :], op=ALU.mult))
            if final_dst is None:
                self.op(ACT, [tfb, self.const_b], [self.HT_b[c][tg]],
                        lambda e, tf=tf, c=c: e.activation(out=self.HT[:, c, sl], in_=tf[:, :], func=AF.Identity,
                                                           scale=self.mod_ap(l, jG, c, s), bias=self.mod_ap(l, jSH, c, s)))
            else:
                YF, yfb = final_dst
                self.op(ACT, [tfb, self.const_b], [yfb],
                        lambda e, tf=tf, c=c: e.activation(out=YF[:, c, :], in_=tf[:, :], func=AF.Copy,
                                                           scale=self.vecT[:, V_GFIN + c:V_GFIN + c + 1]))

    def resid(self, bk, fo, tg, l, jGT, s):
        sl = slice(tg * 512, (tg + 1) * 512)
        self.op(self.DVE, [self.ps_b[bk], self.XT_b[fo][tg], self.const_b], [self.XT_b[fo][tg]],
                lambda e: e.scalar_tensor_tensor(out=self.XT[:, fo, sl], in0=self.ps[bk][:, :], scalar=self.mod_ap(l, jGT, fo, s),
                                                 op0=ALU.mult, in1=self.XT[:, fo, sl], op1=ALU.add))

    def proj_out(self, groups, act_ap, act_b, l, jGT, s, hook=True):
        PE = self.PE
        for gi, grp in enumerate(groups):
            last = (gi == len(groups) - 1)
            for tg in range(TG):
                for fo in range(8):
                    bk = self.nb()

                    def mm(e, grp=grp, tg=tg, fo=fo, bk=bk):
                        ins = None
                        n = len(grp)
                        for i, (wk, k, _) in enumerate(grp):
                            ins = e.matmul(self.ps[bk][:, :], wk[:, fo * 128:(fo + 1) * 128], act_ap(k, tg), start=(i == 0), stop=(i == n - 1))
                        return ins

                    rd = [act_b(k, tg) for (_, k, _) in grp] + list({id(b): b for (_, _, b) in grp}.values())
                    self.op(PE, rd, [self.ps_b[bk]], mm)
                    self.resid(bk, fo, tg, l, jGT, s)
                    if last and hook and fo == 2:
                        self.flush_pending()
                if last and hook and self.next_norm is not None:
                    self.flush_pending()
                    self.pending = (self.next_norm, tg)

    def ffn(self, l, s):
        PE, ACT, DVE = self.PE, self.ACT, self.DVE
        hb = self.fenced(22 * TG)
        HID = self.scr_bf(0, 22 * T).rearrange("p (f t) -> p f t", f=22)
        widths = [512, 512, 512, 512, 512, 256]
        c0 = 0
        for wdt in widths:
            nf = wdt // 128
            slot_a = self.load_slab([(0, 8, wdt, self.w_ffn_in[l][:, c0:c0 + wdt])])
            slot_b = self.load_slab([(0, 8, wdt, self.w_ffn_in[l][:, DFF + c0:DFF + c0 + wdt])])
            wa = self.wv(slot_a, 0, 8, wdt)
            wb = self.wv(slot_b, 0, 8, wdt)
            for tg in range(TG):
                for q in range(nf):
                    f = c0 // 128 + q
                    sl = slice(tg * 512, (tg + 1) * 512)
                    ba, bb = self.nb(), self.nb()
                    for (w, slot, bk) in ((wa, slot_a, ba), (wb, slot_b, bb)):
                        def mm(e, w=w, bk=bk, q=q, sl=sl):
                            ins = None
                            for k in range(8):
                                ins = e.matmul(self.ps[bk][:, :], w[:, k, q * 128:(q + 1) * 128], self.HT[:, k, sl], start=(k == 0), stop=(k == 7))
                            return ins
                        self.op(PE, [self.ring_b[slot]] + self.htb(tg), [self.ps_b[bk]], mm)
                    self.flush_pending()
                    tf, tfb = self.tf_next()
                    self.op(ACT, [self.ps_b[ba]], [tfb], lambda e, tf=tf, ba=ba: e.activation(out=tf[:, :], in_=self.ps[ba][:, :], func=AF.Silu))
                    self.op(DVE, [tfb, self.ps_b[bb]], [hb[f * TG + tg]],
                            lambda e, tf=tf, bb=bb, f=f, sl=sl: e.tensor_tensor(out=HID[:, f, sl], in0=tf[:, :], in1=self.ps[bb][:, :], op=ALU.mult))
            c0 += wdt
        groups = []
        for (k0, nks) in ((0, [4, 4, 4]), (12, [4, 4, 2])):
            grp = []
            k = k0
            for nk in nks:
                slot = self.load_slab([(0, nk, 1024, self.w_ffn_out[l][k * 128:(k + nk) * 128, :])])
                w = self.wv(slot, 0, nk, 1024)
                for i in range(nk):
                    grp.append((w[:, i, :], k + i, self.ring_b[slot]))
                k += nk
            groups.append(grp)
        self.proj_out(groups, lambda k, tg: HID[:, k, tg * 512:(tg + 1) * 512], lambda k, tg: hb[k * TG + tg], l, 5, s)

    def sg_mixer(self, l, j, s):
        PE, ACT, DVE = self.PE, self.ACT, self.DVE
        fb = self.fenced(4 + 16 * TG)
        vb = fb[0:4]
        mxb = fb[4:]
        VB = self.scr_bf(0, 8192).rearrange("p (b f) -> p b f", b=4)
        MX = self.scr_bf(8192, 16 * T).rearrange("p (c t) -> p c t", c=16)
        vs = []
        for i in range(4):
            vs.append(self.load_slab([(0, 8, 512, self.sg_w_in[j][:, 2048 + i * 512:2048 + (i + 1) * 512])]))
        stat = self.stat
        sbq = self.fenced(2)

        deferred = []

        def v_phase(qi):
            p = qi % 2
            base = p * 24
            for b2 in range(2):
                blk = qi * 2 + b2
                slot4 = blk % 4
                tgb = blk // 4
                for fs in range(4):
                    bk = self.nb()
                    w = self.wv(vs[fs], 0, 8, 512)

                    def mm(e, w=w, bk=bk, blk=blk):
                        ins = None
                        for k in range(8):
                            ins = e.matmul(self.ps[bk][:, :], self.HT[:, k, blk * 128:(blk + 1) * 128], w[:, k, :], start=(k == 0), stop=(k == 7))
                        return ins

                    self.op(PE, [self.ring_b[vs[fs]]] + self.htb(tgb), [self.ps_b[bk]], mm)
                    self.flush_pending()
                    c1 = base + b2 * 4 + fs
                    self.op(ACT, [self.ps_b[bk]], [vb[slot4], sbq[p]],
                            lambda e, bk=bk, slot4=slot4, fs=fs, c1=c1: e.activation(out=VB[:, slot4, fs * 512:(fs + 1) * 512], in_=self.ps[bk][:, :], func=AF.Gelu,
                                                                                    accum_out=stat[:, c1:c1 + 1]))
                    if deferred:
                        deferred.pop()()

                    def sq(slot4=slot4, fs=fs, c1=c1, p=p):
                        tb, tbb = self.tb_next()
                        self.op(ACT, [vb[slot4]], [tbb, sbq[p]],
                                lambda e, tb=tb: e.activation(out=tb[:, :], in_=VB[:, slot4, fs * 512:(fs + 1) * 512], func=AF.Square,
                                                              accum_out=stat[:, c1 + 8:c1 + 9]))
                    deferred.append(sq)
            if deferred:
                deferred.pop()()

        def stat_phase(qi):
            p = qi % 2
            base = p * 24
            sb = sbq[p]
            X = mybir.AxisListType.X
            self.op(DVE, [sb], [sb], lambda e: e.tensor_reduce(out=stat[:, base + 16:base + 18], in_=stat[:, base:base + 8].rearrange("p (b f) -> p b f", f=4), axis=X, op=ALU.add))
            self.op(DVE, [sb], [sb], lambda e: e.tensor_reduce(out=stat[:, base + 18:base + 20], in_=stat[:, base + 8:base + 16].rearrange("p (b f) -> p b f", f=4), axis=X, op=ALU.add))
            self.op(DVE, [sb], [sb], lambda e: e.tensor_scalar(out=stat[:, base + 16:base + 20], in0=stat[:, base + 16:base + 20], scalar1=1.0 / 2048, scalar2=None, op0=ALU.mult))
            self.op(DVE, [sb], [sb], lambda e: e.tensor_tensor(out=stat[:, base + 22:base + 24], in0=stat[:, base + 16:base + 18], in1=stat[:, base + 16:base + 18], op=ALU.mult))
            self.op(DVE, [sb], [sb], lambda e: e.tensor_tensor(out=stat[:, base + 18:base + 20], in0=stat[:, base + 18:base + 20], in1=stat[:, base + 22:base + 24], op=ALU.subtract))
            self.op(ACT, [sb], [sb], lambda e: e.activation(out=stat[:, base + 22:base + 24], in_=stat[:, base + 18:base + 20], func=AF.Sqrt, bias=self.epsc[:, 0:1]))
            self.op(DVE, [sb], [sb], lambda e: e.reciprocal(out=stat[:, base + 20:base + 22], in_=stat[:, base + 22:base + 24]))

        def mix_phase(qi):
            p = qi % 2
            base = p * 24
            for b2 in range(2):
                blk = qi * 2 + b2
                slot4 = blk % 4
                tgb = blk // 4
                self.op(DVE, [vb[slot4], sbq[p]], [vb[slot4]],
                        lambda e, slot4=slot4, b2=b2: e.tensor_scalar(out=VB[:, slot4, :], in0=VB[:, slot4, :], scalar1=stat[:, base + 16 + b2:base + 17 + b2],
                                                                     scalar2=stat[:, base + 20 + b2:base + 21 + b2], op0=ALU.subtract, op1=ALU.mult))
                for c4 in range(4):
                    bk = self.nb()

                    def mm2(e, bk=bk, slot4=slot4, c4=c4):
                        ins = None
                        for q in range(4):
                            cc = c4 * 4 + q
                            g = cc // 2
                            ins = e.matmul(self.ps[bk][:, q * 128:(q + 1) * 128], VB[:, slot4, cc * 128:(cc + 1) * 128], self.WsT[:, j * 8 + g, :], start=True, stop=True)
                        return ins

                    self.op(PE, [vb[slot4], self.const_b], [self.ps_b[bk]], mm2)
                    for q in range(4):
                        cc = c4 * 4 + q
                        g = cc // 2
                        self.op(DVE, [self.ps_b[bk], self.const_b], [mxb[cc * TG + tgb]],
                                lambda e, bk=bk, q=q, cc=cc, g=g, blk=blk: e.scalar_tensor_tensor(
                                    out=MX[:, cc, blk * 128:(blk + 1) * 128], in0=self.ps[bk][:, q * 128:(q + 1) * 128],
                                    scalar=self.vecT[:, V_SGG + j * 16 + cc:V_SGG + j * 16 + cc + 1], op0=ALU.mult,
                                    in1=self.Btile[:, j * 8 + g, :], op1=ALU.add))

        NQ = NB // 2
        v_phase(0)
        for qi in range(NQ):
            stat_phase(qi)
            if qi + 1 < NQ:
                v_phase(qi + 1)
            mix_phase(qi)
        for i in range(4):
            slot = self.load_slab([(0, 8, 512, self.sg_w_in[j][:, i * 512:(i + 1) * 512])])
            w = self.wv(slot, 0, 8, 512)
            for q in range(4):
                cc = i * 4 + q
                for tg in range(TG):
                    sl = slice(tg * 512, (tg + 1) * 512)
                    bk = self.nb()

                    def mm(e, w=w, bk=bk, q=q, sl=sl):
                        ins = None
                        for k in range(8):
                            ins = e.matmul(self.ps[bk][:, :], w[:, k, q * 128:(q + 1) * 128], self.HT[:, k, sl], start=(k == 0), stop=(k == 7))
                        return ins

                    self.op(PE, [self.ring_b[slot]] + self.htb(tg), [self.ps_b[bk]], mm)
                    tb, tbb = self.tb_next()
                    self.op(ACT, [self.ps_b[bk]], [tbb], lambda e, tb=tb, bk=bk: e.activation(out=tb[:, :], in_=self.ps[bk][:, :], func=AF.Gelu))
                    self.op(DVE, [tbb, mxb[cc * TG + tg]], [mxb[cc * TG + tg]],
                            lambda e, tb=tb, cc=cc, sl=sl: e.tensor_tensor(out=MX[:, cc, sl], in0=MX[:, cc, sl], in1=tb[:, :], op=ALU.mult))
        grp = []
        for i in range(4):
            slot = self.load_slab([(0, 4, 1024, self.sg_w_out[j][i * 512:(i + 1) * 512, :])])
            w = self.wv(slot, 0, 4, 1024)
            for q in range(4):
                grp.append((w[:, q, :], i * 4 + q, self.ring_b[slot]))
        self.proj_out([grp], lambda k, tg: MX[:, k, tg * 512:(tg + 1) * 512], lambda k, tg: mxb[k * TG + tg], l, 2, s)

    def pool_mixer(self, l, s, hf):
        PE, ACT, DVE = self.PE, self.ACT, self.DVE
        fb = self.fenced(2 + 8 * TG + 8 * TG)
        zb = fb[0:2]
        ptb = fb[2:2 + 8 * TG]
        y2b = fb[2 + 8 * TG:]
        ZB = self.scr_bf(0, 2048).rearrange("p (i f) -> p i f", i=2)
        PT = self.scr_bf(2048, 8 * T).rearrange("p (c t) -> p c t", c=8)
        Y2 = self.scr_bf(2048 + 8 * T, 8 * T).rearrange("p (c t) -> p c t", c=8)
        ws = [self.load_slab([(0, 8, 512, self.pool_w_in[:, i * 512:(i + 1) * 512])]) for i in range(2)]
        for blk in range(NB):
            tgb = blk // 4
            cur = blk % 2
            for fs in range(2):
                bk = self.nb()
                w = self.wv(ws[fs], 0, 8, 512)

                def mm(e, w=w, bk=bk, blk=blk):
                    ins = None
                    for k in range(8):
                        ins = e.matmul(self.ps[bk][:, :], self.HT[:, k, blk * 128:(blk + 1) * 128], w[:, k, :], start=(k == 0), stop=(k == 7))
                    return ins

                self.op(PE, [self.ring_b[ws[fs]]] + self.htb(tgb), [self.ps_b[bk]], mm)
                self.flush_pending()
                if fs == 0:
                    self.op(ACT, [self.ps_b[bk]], [zb[cur]], lambda e, bk=bk, cur=cur: e.activation(out=ZB[:, cur, 0:512], in_=self.ps[bk][:, :], func=AF.Copy))
                else:
                    self.op(DVE, [self.ps_b[bk]], [zb[cur]], lambda e, bk=bk, cur=cur: e.tensor_copy(out=ZB[:, cur, 512:1024], in_=self.ps[bk][:, :]))
            first = (hf == 0 and blk == 0)
            if blk == 0:
                prev_ap = lambda cc: self.ZC[:, cc * 128:(cc + 1) * 128]
                prev_b = self.ZC_b
            else:
                prev_ap = lambda cc, cur=cur: ZB[:, 1 - cur, cc * 128:(cc + 1) * 128]
                prev_b = zb[1 - cur]
            for h2 in range(2):
                bk = self.nb()

                def mm3(e, bk=bk, h2=h2, cur=cur, first=first, prev_ap=prev_ap):
                    ins = None
                    for q in range(4):
                        cc = h2 * 4 + q
                        g = cc // 2
                        o = self.ps[bk][:, q * 128:(q + 1) * 128]
                        if first:
                            ins = e.matmul(o, ZB[:, cur, cc * 128:(cc + 1) * 128], self.poolA[:, 8 + g, :], start=True, stop=True)
                        else:
                            e.matmul(o, ZB[:, cur, cc * 128:(cc + 1) * 128], self.poolA[:, g, :], start=True, stop=False)
                            ins = e.matmul(o, prev_ap(cc), self.poolA[:, 4 + g, :], start=False, stop=True)
                    return ins

                rd = [zb[cur], self.const_b] + ([] if first else [prev_b])
                self.op(PE, rd, [self.ps_b[bk]], mm3)
                wr = [ptb[(h2 * 4 + q) * TG + tgb] for q in range(4)]
                dst = PT[:, h2 * 4:h2 * 4 + 4, blk * 128:(blk + 1) * 128]
                src = self.ps[bk][:, :].rearrange("p (a t) -> p a t", a=4)
                if h2 == 0:
                    self.op(ACT, [self.ps_b[bk]], wr, lambda e, dst=dst, src=src: e.activation(out=dst, in_=src, func=AF.Copy))
                else:
                    self.op(DVE, [self.ps_b[bk]], wr, lambda e, dst=dst, src=src: e.tensor_copy(out=dst, in_=src))
            if blk == NB - 1 and hf < UPS - 1:
                self.op(DVE, [zb[cur]], [self.ZC_b], lambda e, cur=cur: e.tensor_copy(out=self.ZC[:, :], in_=ZB[:, cur, :]))
        slot = self.load_slab([(0, 8, 256, self.pool_w_grp.rearrange("g c d -> (g c) d"))])
        wg = self.wv(slot, 0, 8, 256)
        for dd in range(8):
            g, dc = dd // 2, dd % 2
            for tg in range(TG):
                sl = slice(tg * 512, (tg + 1) * 512)
                bk = self.nb()

                def mm(e, bk=bk, g=g, dc=dc, sl=sl):
                    e.matmul(self.ps[bk][:, :], wg[:, g * 2, dc * 128:(dc + 1) * 128], PT[:, g * 2, sl], start=True, stop=False)
                    return e.matmul(self.ps[bk][:, :], wg[:, g * 2 + 1, dc * 128:(dc + 1) * 128], PT[:, g * 2 + 1, sl], start=False, stop=True)

                self.op(PE, [self.ring_b[slot], ptb[(g * 2) * TG + tg], ptb[(g * 2 + 1) * TG + tg]], [self.ps_b[bk]], mm)
                self.op(ACT, [self.ps_b[bk], self.const_b], [y2b[dd * TG + tg]],
                        lambda e, bk=bk, dd=dd, sl=sl: e.activation(out=Y2[:, dd, sl], in_=self.ps[bk][:, :], func=AF.Identity,
                                                                    scale=self.vecT[:, V_PS + dd:V_PS + dd + 1], bias=self.PBS[:, dd:dd + 1]))
        grp = []
        for i in range(2):
            slot = self.load_slab([(0, 4, 1024, self.pool_w_out[i * 512:(i + 1) * 512, :])])
            w = self.wv(slot, 0, 4, 1024)
            for q in range(4):
                grp.append((w[:, q, :], i * 4 + q, self.ring_b[slot]))
        self.proj_out([grp], lambda k, tg: Y2[:, k, tg * 512:(tg + 1) * 512], lambda k, tg: y2b[k * TG + tg], l, 2, s)

    def ret_mixer(self, l, s, hf):
        PE, ACT, DVE, SPQ = self.PE, self.ACT, self.DVE, self.SPQ
        nfb = 2 * TG + 2 * TG + NB + NB + 4 * TG + 1
        fb = self.fenced(nfb)
        qb = fb[0:2 * TG]
        kb = fb[2 * TG:4 * TG]
        vb = fb[4 * TG:4 * TG + NB]
        kdb = fb[4 * TG + NB:4 * TG + 2 * NB]
        gsb = fb[4 * TG + 2 * NB:4 * TG + 2 * NB + 4 * TG]
        obb = vb
        rb = fb[-1]
        o = 0
        QT = self.scr_bf(o, 2 * T).rearrange("p (c t) -> p c t", c=2); o += 2 * T
        KT = self.scr_bf(o, 2 * T).rearrange("p (c t) -> p c t", c=2); o += 2 * T
        V = self.scr_bf(o, NB * 512).rearrange("p (b e) -> p b e", b=NB); o += NB * 512
        OB = V
        KD = self.scr_bf(o, NB * 256).rearrange("p (b d) -> p b d", b=NB); o += NB * 256
        GS = self.scr_bf(o, 4 * T).rearrange("p (c t) -> p c t", c=4); o += 4 * T
        ROPE = self.scr_f32(o, 4 * T).rearrange("p (a t) -> p a t", a=2)
        o += 4 * T
        assert o <= SCR
        self.dma(SPQ, self.rope_sem, [], [rb],
                 lambda e: [e.dma_start(out=ROPE, in_=self.c_rope[:, :, hf * T:(hf + 1) * T].rearrange("a p t -> p a t"))])
        stat = self.stat
        sb = self.stat_b
        if hf == 0:
            for i in range(8):
                self.op(DVE, [], [self.S_b[i]], lambda e, i=i: e.memset(self.S32[:, i, :], 0.0))
                self.op(DVE, [], [self.SBF_b[i]], lambda e, i=i: e.memset(self.SBF[:, i, :], 0.0))
        Win = self.ret_w_in
        for h in range(4):
            gam = 1.0 - 2.0 ** (-5 - h)
            slot = self.load_slab([(0, 8, 256, Win[:, h * 256:(h + 1) * 256]), (2048, 8, 256, Win[:, 1024 + h * 256:1024 + (h + 1) * 256])])
            for tg in range(TG):
                for (which, off, DST, dbufs) in ((0, 0, QT, qb), (1, 2048, KT, kb)):
                    w = self.wv(slot, off, 8, 256)
                    sl = slice(tg * 512, (tg + 1) * 512)
                    b1, b2 = self.nb(), self.nb()
                    for (dc, bk) in ((0, b1), (1, b2)):
                        def mm(e, w=w, bk=bk, dc=dc, sl=sl):
                            ins = None
                            for k in range(8):
                                ins = e.matmul(self.ps[bk][:, :], w[:, k, dc * 128:(dc + 1) * 128], self.HT[:, k, sl], start=(k == 0), stop=(k == 7))
                            return ins
                        self.op(PE, [self.ring_b[slot]] + self.htb(tg), [self.ps_b[bk]], mm)
                    self.flush_pending()
                    cosv = ROPE[:, 0, sl]
                    sinv = ROPE[:, 1, sl]
                    t1, t1b = self.tf_next()
                    t2, t2b = self.tf_next()
                    self.op(DVE, [self.ps_b[b1], rb], [t1b], lambda e, t1=t1, b1=b1, cosv=cosv: e.tensor_tensor(out=t1[:, :], in0=self.ps[b1][:, :], in1=cosv, op=ALU.mult))
                    self.op(DVE, [self.ps_b[b2], rb], [t2b], lambda e, t2=t2, b2=b2, sinv=sinv: e.tensor_tensor(out=t2[:, :], in0=self.ps[b2][:, :], in1=sinv, op=ALU.mult))
                    self.op(DVE, [t1b, t2b], [dbufs[0 * TG + tg]], lambda e, t1=t1, t2=t2, DST=DST, sl=sl: e.tensor_tensor(out=DST[:, 0, sl], in0=t1[:, :], in1=t2[:, :], op=ALU.subtract))
                    t3, t3b = self.tf_next()
                    t4, t4b = self.tf_next()
                    self.op(DVE, [self.ps_b[b2], rb], [t3b], lambda e, t3=t3, b2=b2, cosv=cosv: e.tensor_tensor(out=t3[:, :], in0=self.ps[b2][:, :], in1=cosv, op=ALU.mult))
                    self.op(DVE, [self.ps_b[b1], rb], [t4b], lambda e, t4=t4, b1=b1, sinv=sinv: e.tensor_tensor(out=t4[:, :], in0=self.ps[b1][:, :], in1=sinv, op=ALU.mult))
                    self.op(DVE, [t3b, t4b], [dbufs[1 * TG + tg]], lambda e, t3=t3, t4=t4, DST=DST, sl=sl: e.tensor_tensor(out=DST[:, 1, sl], in0=t3[:, :], in1=t4[:, :], op=ALU.add))
            slot_v = self.load_slab([(0, 8, 512, Win[:, 2048 + h * 512:2048 + (h + 1) * 512])])
            wvv = self.wv(slot_v, 0, 8, 512)
            for blk in range(NB):
                bk = self.nb()

                def mm(e, bk=bk, blk=blk):
                    ins = None
                    for k in range(8):
                        ins = e.matmul(self.ps[bk][:, :], self.HT[:, k, blk * 128:(blk + 1) * 128], wvv[:, k, :], start=(k == 0), stop=(k == 7))
                    return ins

                self.op(PE, [self.ring_b[slot_v]] + self.htb(blk // 4), [self.ps_b[bk]], mm)
                self.op(ACT, [self.ps_b[bk]], [vb[blk]], lambda e, bk=bk, blk=blk: e.activation(out=V[:, blk, :], in_=self.ps[bk][:, :], func=AF.Copy))
            for blk in range(NB):
                bk = self.nb()
                pbf = self.ps[bk][:, :].bitcast(BF16)

                def tr(e, blk=blk, pbf=pbf):
                    e.transpose(pbf[:, 0:128], KT[:, 0, blk * 128:(blk + 1) * 128], self.identb[:, :])
                    return e.transpose(pbf[:, 128:256], KT[:, 1, blk * 128:(blk + 1) * 128], self.identb[:, :])

                self.op(PE, [kb[0 * TG + blk // 4], kb[1 * TG + blk // 4], self.const_b], [self.ps_b[bk]], tr)
                self.op(ACT, [self.ps_b[bk], self.const_b], [kdb[blk]],
                        lambda e, blk=blk, pbf=pbf, h=h: e.activation(out=KD[:, blk, :], in_=pbf[:, 0:256], func=AF.Copy, scale=self.rvec[:, h:h + 1]))
            slot_g = self.load_slab([(0, 8, 512, Win[:, 4096 + h * 512:4096 + (h + 1) * 512])])
            wg = self.wv(slot_g, 0, 8, 512)
            g_jobs = [(gc, tg) for tg in range(TG) for gc in range(4)]

            def g_group(gc, tg, wg=wg, slot_g=slot_g):
                sl = slice(tg * 512, (tg + 1) * 512)
                bk = self.nb()

                def mm(e, bk=bk, gc=gc, sl=sl):
                    ins = None
                    for k in range(8):
                        ins = e.matmul(self.ps[bk][:, :], wg[:, k, gc * 128:(gc + 1) * 128], self.HT[:, k, sl], start=(k == 0), stop=(k == 7))
                    return ins

                self.op(PE, [self.ring_b[slot_g]] + self.htb(tg), [self.ps_b[bk]], mm)
                self.op(ACT, [self.ps_b[bk]], [gsb[gc * TG + tg]], lambda e, bk=bk, gc=gc, sl=sl: e.activation(out=GS[:, gc, sl], in_=self.ps[bk][:, :], func=AF.Silu))

            for blk in range(NB):
                tgb = blk // 4
                bsl = slice(blk * 128, (blk + 1) * 128)
                bks = self.nb()

                def mms(e, bks=bks, bsl=bsl):
                    e.matmul(self.ps[bks][:, 0:128], KT[:, 0, bsl], QT[:, 0, bsl], start=True, stop=False)
                    return e.matmul(self.ps[bks][:, 0:128], KT[:, 1, bsl], QT[:, 1, bsl], start=False, stop=True)

                self.op(PE, [kb[tgb], kb[TG + tgb], qb[tgb], qb[TG + tgb]], [self.ps_b[bks]], mms)
                pm, pmb = self.tb_next()
                self.op(DVE, [self.ps_b[bks], self.const_b], [pmb],
                        lambda e, pm=pm, bks=bks, h=h: e.tensor_tensor(out=pm[:, 0:128], in0=self.ps[bks][:, 0:128], in1=self.dmask[:, h, :], op=ALU.mult))
                for _ in range((len(g_jobs) + (NB - blk) - 1) // (NB - blk)):
                    g_group(*g_jobs.pop(0))
                bko = self.nb()

                def mmo(e, bko=bko, pm=pm, blk=blk, bsl=bsl, h=h):
                    e.matmul(self.ps[bko][:, :], pm[:, 0:128], V[:, blk, :], start=True, stop=False)
                    e.matmul(self.ps[bko][:, :], QT[:, 0, bsl], self.SBF[:, h * 2, :], start=False, stop=False)
                    return e.matmul(self.ps[bko][:, :], QT[:, 1, bsl], self.SBF[:, h * 2 + 1, :], start=False, stop=True)

                self.op(PE, [pmb, vb[blk], qb[tgb], qb[TG + tgb], self.SBF_b[h * 2], self.SBF_b[h * 2 + 1]], [self.ps_b[bko]], mmo)
                for dc in range(2):
                    bkd = self.nb()
                    self.op(PE, [kdb[blk], vb[blk]], [self.ps_b[bkd]],
                            lambda e, bkd=bkd, blk=blk, dc=dc: e.matmul(self.ps[bkd][:, :], KD[:, blk, dc * 128:(dc + 1) * 128], V[:, blk, :], start=True, stop=True))
                    i = h * 2 + dc
                    self.op(DVE, [self.ps_b[bkd], self.S_b[i]], [self.S_b[i]],
                            lambda e, bkd=bkd, i=i, gam=gam: e.scalar_tensor_tensor(out=self.S32[:, i, :], in0=self.S32[:, i, :], scalar=float(gam ** 128), op0=ALU.mult,
                                                                                  in1=self.ps[bkd][:, :], op1=ALU.add))
                    self.op(ACT, [self.S_b[i]], [self.SBF_b[i]], lambda e, i=i: e.activation(out=self.SBF[:, i, :], in_=self.S32[:, i, :], func=AF.Copy))
                tj, tjb = self.tf_next()
                self.op(ACT, [self.ps_b[bko]], [tjb, sb],
                        lambda e, tj=tj, bko=bko, blk=blk: e.activation(out=tj[:, :], in_=self.ps[bko][:, :], func=AF.Square, accum_out=stat[:, 48 + blk:49 + blk]))
                self.op(DVE, [self.ps_b[bko]], [obb[blk]], lambda e, bko=bko, blk=blk: e.tensor_copy(out=OB[:, blk, :], in_=self.ps[bko][:, :]))
            while g_jobs:
                g_group(*g_jobs.pop(0))
            self.op(ACT, [sb, self.const_b], [sb], lambda e, h=h: e.activation(out=stat[:, 56:56 + NB], in_=stat[:, 48:48 + NB], func=AF.Sqrt, scale=1.0 / 512, bias=self.rvec[:, 4 + h:5 + h]))
            self.op(DVE, [sb], [sb], lambda e: e.reciprocal(out=stat[:, 48:48 + NB], in_=stat[:, 56:56 + NB]))
            for blk in range(NB):
                self.op(DVE, [obb[blk], sb], [obb[blk]],
                        lambda e, blk=blk: e.tensor_scalar(out=OB[:, blk, :], in0=OB[:, blk, :], scalar1=stat[:, 48 + blk:49 + blk], scalar2=None, op0=ALU.mult))
            for blk in range(NB):
                tgb = blk // 4
                bk = self.nb()
                pbf = self.ps[bk][:, :].bitcast(BF16)

                def tr(e, blk=blk, pbf=pbf):
                    ins = None
                    for ec in range(4):
                        ins = e.transpose(pbf[:, ec * 128:(ec + 1) * 128], OB[:, blk, ec * 128:(ec + 1) * 128], self.identb[:, :])
                    return ins

                self.op(PE, [obb[blk], self.const_b], [self.ps_b[bk]], tr)
                wr = [gsb[ec * TG + tgb] for ec in range(4)]
                dst = GS[:, :, blk * 128:(blk + 1) * 128]
                self.op(DVE, [self.ps_b[bk]] + wr, wr,
                        lambda e, dst=dst, pbf=pbf: e.tensor_tensor(out=dst, in0=pbf[:, 0:512].rearrange("p (a t) -> p a t", a=4), in1=dst, op=ALU.mult))
            slot_o = self.load_slab([(0, 4, 1024, self.ret_w_out[h * 512:(h + 1) * 512, :])])
            wo = self.wv(slot_o, 0, 4, 1024)
            grp = [(wo[:, q, :], q, self.ring_b[slot_o]) for q in range(4)]
            self.proj_out([grp], lambda k, tg: GS[:, k, tg * 512:(tg + 1) * 512], lambda k, tg: gsb[k * TG + tg], l, 2, s, hook=(h == 3))


def _consts():
    ident = np.eye(128, dtype=np.float32)
    A = np.zeros((12, 128, 128), np.float32)
    tp = np.arange(128)[:, None]
    t = np.arange(128)[None, :]
    for g, win in enumerate((2, 4, 8, 16)):
        d = t - tp
        A[g] = ((d >= 0) & (d < win)).astype(np.float32) / win - (d == 0).astype(np.float32)
        dp = t - (tp - 128)
        A[4 + g] = ((dp >= 0) & (dp < win)).astype(np.float32) / win
        cnt = np.minimum(t + 1, win).astype(np.float32)
        A[8 + g] = ((d >= 0) & (d < win)).astype(np.float32) / cnt - (d == 0).astype(np.float32)
    inv = (10000.0 ** (-np.arange(0, 256, 2, dtype=np.float32) / np.float32(256))).astype(np.float32)
    pos = np.arange(SEQ, dtype=np.float32)
    ang = (pos[:, None] * inv[None, :]).astype(np.float32)
    rope = np.stack([np.cos(ang).T, np.sin(ang).T]).astype(np.float32)
    dm = np.zeros((4, 128, 128), np.float64)
    rvec = np.zeros((128, 8), np.float64)
    idx = np.arange(128, dtype=np.float64)
    for h in range(4):
        gam = 1.0 - 2.0 ** (-5 - h)
        lg = np.log(gam)
        c = idx[None, :]
        m = idx[:, None]
        same = (np.floor(c / 64) == np.floor(m / 64))
        past = (np.floor(m / 64) < np.floor(c / 64))
        Dm = np.where(same, np.exp(lg * np.abs(c - m)), np.where(past, np.exp(lg * (c - m)), 0.0))
        qd = np.exp(lg * (idx + 1.0))
        dm[h] = Dm / qd[None, :] / 16.0
        rvec[:, h] = np.exp(lg * (127.0 - idx)) / 16.0
        rvec[:, 4 + h] = EPS / qd ** 2
    jj = np.arange(128)[:, None]
    ii = np.arange(128)[None, :]
    sgm = ((ii // 64) >= (jj // 64)).astype(np.float32)
    return dict(c_ident=ident, c_poolA=A, c_rope=np.ascontiguousarray(rope), c_dmask=dm.astype(np.float32),
                c_rvec=rvec.astype(np.float32), c_sgmask=sgm)


def _pack_vecs(inp):
    v = np.zeros((V_ROWS, 128), np.float32)
    v[V_GMIX:V_GMIX + 32] = np.asarray(inp["norm_mix_g"], np.float32).reshape(32, 128)
    v[V_GFFN:V_GFFN + 32] = np.asarray(inp["norm_ffn_g"], np.float32).reshape(32, 128)
    v[V_BADA:V_BADA + 192] = np.asarray(inp["b_ada"], np.float32).reshape(192, 128)
    v[V_GFIN:V_GFIN + 8] = np.asarray(inp["final_norm_g"], np.float32).reshape(8, 128)
    v[V_SGG:V_SGG + 32] = np.asarray(inp["sg_v_norm_g"], np.float32).reshape(32, 128)
    v[V_PB:V_PB + 8] = np.asarray(inp["pool_b_grp"], np.float32).reshape(8, 128)
    v[V_PS:V_PS + 8] = np.asarray(inp["pool_scale"], np.float32).reshape(8, 128)
    return v


_NC_CACHE = {}


def _weights(inp, layers, do_mixer, do_ffn):
    f = lambda a: np.ascontiguousarray(np.asarray(a, np.float32))
    w = {}
    for l in layers:
        w[f"w_ada{l}"] = f(inp["w_ada"][l])
        if do_ffn:
            w[f"w_ffn_in{l}"] = f(inp["w_ffn_in"][l])
            w[f"w_ffn_out{l}"] = f(inp["w_ffn_out"][l])
        if do_mixer and l % 3 == 0:
            w[f"sg_w_in{l // 3}"] = f(inp["sg_w_in"][l // 3])
            w[f"sg_w_out{l // 3}"] = f(inp["sg_w_out"][l // 3])
        if do_mixer and l == 1:
            w["pool_w_in"] = f(inp["pool_w_in"][0])
            w["pool_w_grp"] = f(inp["pool_w_grp"][0])
            w["pool_w_out"] = f(inp["pool_w_out"][0])
        if do_mixer and l == 2:
            w["ret_w_in"] = f(inp["ret_w_in"][0])
            w["ret_w_out"] = f(inp["ret_w_out"][0])
    w["sg_w_s"] = f(inp["sg_w_s"])
    w["sg_b_s"] = f(inp["sg_b_s"])
    return w


def run_layers(inp, x, layers, final, do_mixer=True, do_ffn=True, trace=False, ncores=NCORES):
    B = x.shape[0]
    nseq = B // ncores
    key = (nseq, tuple(layers), final, do_mixer, do_ffn)
    if key not in _NC_CACHE:
        _NC_CACHE[key] = Gen(nseq, list(layers), final, do_mixer, do_ffn).build()
    nc = _NC_CACHE[key]
    consts = _consts()
    vecs = _pack_vecs(inp)
    shared = _weights(inp, layers, do_mixer, do_ffn)
    c = np.asarray(inp["c"], np.float32)
    in_maps = []
    for core in range(ncores):
        xs = np.ascontiguousarray(x[core * nseq:(core + 1) * nseq].reshape(nseq * SEQ, D))
        cs = c[core * nseq:(core + 1) * nseq]
        ct = np.ascontiguousarray(cs.reshape(nseq, 8, 128).transpose(1, 0, 2).reshape(8 * nseq, 128))
        m = {"x": xs, "ct": ct, "vecs": vecs}
        m.update(shared)
        m.update(consts)
        in_maps.append(m)
    res = run_bass_kernel_spmd(nc, in_maps, core_ids=list(range(ncores)), trace=trace)
    y = np.concatenate([r["y"].reshape(nseq, SEQ, D) for r in res.results], axis=0)
    return y, res


def kernel(**inputs):
    x = np.asarray(inputs["x"], np.float32)
    y, _ = run_layers(inputs, x, [0, 1, 2, 3], True)
    return y.astype(np.float32)
```

```python
import numpy as np
from contextlib import ExitStack
import concourse.bass as bass
import concourse.mybir as mybir
from concourse.bass_utils import run_bass_kernel_spmd

F32 = mybir.dt.float32
BF16 = mybir.dt.bfloat16
AF = mybir.ActivationFunctionType
ALU = mybir.AluOpType

NCORES = 8
D = 1024
SEQ = 2048
BATCH = 32
DEPTH = 4
DFF = 2816
EPS = 1e-6
T = 1024
TG = T // 512
NB = T // 128
UPS = SEQ // T
SLOT = 4096
NSLOT = 6
SCR = 24576

V_GMIX, V_GFFN, V_BADA, V_GFIN, V_SGG, V_PB, V_PS = 0, 32, 64, 256, 264, 296, 304
V_ROWS = 384


class SemC:
    def __init__(self, h):
        self.h = h
        self.count = 0


class Eng:
    def __init__(self, e, sem, is_pe=False):
        self.e = e
        self.sem = sem
        self.is_pe = is_pe
        self.waited = {}


class Buf:
    __slots__ = ("w", "r")

    def __init__(self):
        self.w = None
        self.r = {}


class Gen:
    def __init__(self, nseq, layers, final, do_mixer=True, do_ffn=True):
        self.nseq = nseq
        self.layers = layers
        self.final = final
        self.do_mixer = do_mixer
        self.do_ffn = do_ffn

    def _wait(self, E, reads, writes):
        need = {}
        own = E.sem
        for b in reads:
            if b.w is not None:
                s, v = b.w
                if need.get(s, 0) < v:
                    need[s] = v
        for b in writes:
            if b.w is not None:
                s, v = b.w
                if s is not own and need.get(s, 0) < v:
                    need[s] = v
            for s, v in b.r.items():
                if s is not own and need.get(s, 0) < v:
                    need[s] = v
        for s, v in need.items():
            if E.is_pe and s is own:
                continue
            if E.waited.get(s, 0) >= v:
                continue
            E.e.wait_ge(s.h, v)
            E.waited[s] = v

    def op(self, E, reads, writes, fn):
        self._wait(E, reads, writes)
        ins = fn(E.e)
        E.sem.count += 1
        ins.then_inc(E.sem.h, 1)
        c = E.sem.count
        for b in reads:
            b.r[E.sem] = c
        for b in writes:
            b.w = (E.sem, c)
            b.r = {}

    def dma(self, Q, sem, reads, writes, fn):
        self._wait(Q, reads, writes)
        inss = fn(Q.e)
        for ins in inss:
            sem.count += 16
            ins.then_inc(sem.h, 16)
        c = sem.count
        for b in reads:
            b.r[sem] = c
        for b in writes:
            b.w = (sem, c)
            b.r = {}

    def fenced(self, n):
        snap = {}
        for s in self.all_sems:
            if s.count > 0:
                snap[s] = s.count
        out = []
        for _ in range(n):
            b = Buf()
            b.r = dict(snap)
            out.append(b)
        return out

    def tile(self, name, shape, dt):
        return self.es.enter_context(self.nc.sbuf_tensor(name, shape, dt))

    def sem(self, name):
        s = SemC(self.es.enter_context(self.nc.semaphore(name)))
        self.all_sems.append(s)
        return s

    def flush_pending(self):
        if self.pending is not None:
            args, tg = self.pending
            self.pending = None
            self.norm_mod(*args, tg)

    def htb(self, tg):
        if self.pending is not None and self.pending[1] == tg:
            self.flush_pending()
        return [self.HT_b[k][tg] for k in range(8)]

    def nb(self):
        b = self.bank_i
        self.bank_i = (b + 1) % 8
        return b

    def tf_next(self):
        i = self.tf_i
        self.tf_i = (i + 1) % len(self.TF)
        return self.TF[i], self.TF_b[i]

    def tb_next(self):
        i = self.tb_i
        self.tb_i = (i + 1) % len(self.TB)
        return self.TB[i], self.TB_b[i]

    def scr_bf(self, off, n):
        return self.SCRT[:, off:off + n]

    def scr_f32(self, off, n):
        return self.SCRT[:, off:off + n].bitcast(F32)

    def load_slab(self, pieces):
        slot = self.ring_i
        self.ring_i = (slot + 1) % NSLOT
        rt = self.ring_t[slot]

        def fn(e):
            inss = []
            for (off, nk, ncl, src) in pieces:
                dst = rt[:, off:off + nk * ncl].rearrange("p (k c) -> p k c", k=nk)
                inss.append(e.dma_start(out=dst, in_=src.rearrange("(k p) c -> p k c", p=128)))
            return inss

        self.dma(self.PQ, self.ring_sem[slot], [], [self.ring_b[slot]], fn)
        return slot

    def wv(self, slot, off, nk, ncl):
        return self.ring_t[slot][:, off:off + nk * ncl].rearrange("p (k c) -> p k c", k=nk)

    def build(self):
        nc = bass.Bass("TRN2", target_bir_lowering=False)
        self.nc = nc
        nseq = self.nseq
        NT = nseq * SEQ

        def din(name, shape):
            return nc.dram_tensor(name, list(shape), F32, kind="ExternalInput").ap()

        self.x_d = din("x", (NT, D))
        self.ct_d = din("ct", (8 * nseq, 128))
        self.vec_d = din("vecs", (V_ROWS, 128))
        L = self.layers
        self.w_ada = {l: din(f"w_ada{l}", (D, 6 * D)) for l in L}
        self.w_ffn_in = {l: din(f"w_ffn_in{l}", (D, 2 * DFF)) for l in L if self.do_ffn}
        self.w_ffn_out = {l: din(f"w_ffn_out{l}", (DFF, D)) for l in L if self.do_ffn}
        sgj = sorted({l // 3 for l in L if l % 3 == 0 and self.do_mixer})
        self.sg_w_in = {j: din(f"sg_w_in{j}", (D, 4096)) for j in sgj}
        self.sg_w_out = {j: din(f"sg_w_out{j}", (2048, D)) for j in sgj}
        self.sg_w_s = din("sg_w_s", (2, 8, 128, 128))
        self.sg_b_s = din("sg_b_s", (2, 8, 128))
        if 1 in L and self.do_mixer:
            self.pool_w_in = din("pool_w_in", (D, D))
            self.pool_w_grp = din("pool_w_grp", (4, 256, 256))
            self.pool_w_out = din("pool_w_out", (D, D))
        if 2 in L and self.do_mixer:
            self.ret_w_in = din("ret_w_in", (D, 6144))
            self.ret_w_out = din("ret_w_out", (2048, D))
        self.c_ident = din("c_ident", (128, 128))
        self.c_poolA = din("c_poolA", (12, 128, 128))
        self.c_rope = din("c_rope", (2, 128, SEQ))
        self.c_dmask = din("c_dmask", (4, 128, 128))
        self.c_rvec = din("c_rvec", (128, 8))
        self.c_sgmask = din("c_sgmask", (128, 128))
        self.y_d = nc.dram_tensor("y", [NT, D], F32, kind="ExternalOutput").ap()

        with ExitStack() as es:
            self.es = es
            self.all_sems = []
            self.PE = Eng(nc.tensor, self.sem("s_pe"), is_pe=True)
            self.ACT = Eng(nc.scalar, self.sem("s_act"))
            self.DVE = Eng(nc.vector, self.sem("s_dve"))
            self.SPQ = Eng(nc.sync, None)
            self.PQ = Eng(nc.gpsimd, None)
            self.csem = self.sem("s_const")
            self.xin_sem = [self.sem("s_xin0"), self.sem("s_xin1")]
            self.yout_sem = [self.sem("s_yo0"), self.sem("s_yo1")]
            self.rope_sem = self.sem("s_rope")
            self.ring_sem = [self.sem(f"s_ring{i}") for i in range(NSLOT)]

            self.ps = [es.enter_context(nc.psum_tensor(f"ps{i}", [128, 512], F32)) for i in range(8)]
            self.ps_b = [Buf() for _ in range(8)]
            self.bank_i = 0
            self.pending = None

            self.XT = self.tile("XT", [128, 8, T], F32)
            self.XT_b = [[Buf() for _ in range(TG)] for _ in range(8)]
            self.HT = self.tile("HT", [128, 8, T], BF16)
            self.HT_b = [[Buf() for _ in range(TG)] for _ in range(8)]
            self.ring_t = [self.tile(f"ring{i}", [128, SLOT], BF16) for i in range(NSLOT)]
            self.ring_b = [Buf() for _ in range(NSLOT)]
            self.ring_i = 0
            self.SCRT = self.tile("scr", [128, SCR], BF16)
            self.S32 = self.tile("S32", [128, 8, 512], F32)
            self.SBF = self.tile("SBF", [128, 8, 512], BF16)
            self.S_b = [Buf() for _ in range(8)]
            self.SBF_b = [Buf() for _ in range(8)]
            self.ZC = self.tile("ZC", [128, 1024], BF16)
            self.ZC_b = Buf()
            self.TF = [self.tile(f"tf{i}", [128, 512], F32) for i in range(4)]
            self.TF_b = [Buf() for _ in range(4)]
            self.tf_i = 0
            self.TB = [self.tile(f"tb{i}", [128, 512], BF16) for i in range(3)]
            self.TB_b = [Buf() for _ in range(3)]
            self.tb_i = 0
            self.identf = self.tile("identf", [128, 128], F32)
            self.identb = self.tile("identb", [128, 128], BF16)
            self.onesb = self.tile("onesb", [128, 128], BF16)
            self.poolA = self.tile("poolA", [128, 12, 128], BF16)
            self.dmask = self.tile("dmask", [128, 4, 128], F32)
            self.rvec = self.tile("rvec", [128, 8], F32)
            self.sgmask = self.tile("sgmask", [128, 128], F32)
            self.vecT = self.tile("vecT", [128, V_ROWS], F32)
            self.condT = self.tile("condT", [128, 8 * nseq], BF16)
            self.MOD = self.tile("MOD", [128, DEPTH * 48 * nseq], F32)
            self.WsT = self.tile("WsT", [128, 16, 128], BF16)
            self.Btile = self.tile("Btile", [128, 16, 128], BF16)
            self.PBS = self.tile("PBS", [128, 8], F32)
            self.stat = self.tile("stat", [128, 64], F32)
            self.epsc = self.tile("epsc", [128, 1], F32)
            self.stat_b = Buf()
            self.const_b = Buf()

            self.preamble()
            for s in range(nseq):
                for hf in range(UPS):
                    self.unit(s, hf)
            for i in range(2):
                if self.yout_sem[i].count > 0:
                    nc.sync.wait_ge(self.yout_sem[i].h, self.yout_sem[i].count)
        return nc

    def mod_ap(self, l, j, c, s):
        nseq = self.nseq
        o = ((l * 6 + j) * 8 + c) * nseq + s
        return self.MOD[:, o:o + 1]

    def preamble(self):
        nc = self.nc
        nseq = self.nseq
        PE, ACT, DVE, SPQ, PQ = self.PE, self.ACT, self.DVE, self.SPQ, self.PQ
        cb = self.const_b
        o = 0
        VR = self.scr_f32(o, 2 * 384)
        o += 2 * 384
        CR = self.scr_f32(o, 2 * 128)
        o += 2 * 128
        WS = self.scr_f32(o, 2 * 16 * 128)
        o += 2 * 16 * 128
        sb = Buf()

        def cl(e):
            inss = []
            inss.append(e.dma_start(out=self.identf[:, :], in_=self.c_ident))
            inss.append(e.dma_start(out=self.dmask[:, :, :], in_=self.c_dmask.rearrange("h m c -> m h c")))
            inss.append(e.dma_start(out=self.rvec[:, :], in_=self.c_rvec))
            inss.append(e.dma_start(out=self.sgmask[:, :], in_=self.c_sgmask))
            inss.append(e.dma_start(out=VR.rearrange("p (t f) -> p t f", t=3),
                                    in_=self.vec_d.rearrange("(t p) f -> p t f", p=128)))
            inss.append(e.dma_start(out=CR[0:8 * nseq, :], in_=self.ct_d))
            inss.append(e.dma_start(out=WS.rearrange("p (a j) -> p a j", a=16),
                                    in_=self.sg_w_s.rearrange("l g i j -> i (l g) j")))
            return inss

        self.dma(SPQ, self.csem, [], [cb, sb], cl)

        def cl2(e):
            inss = []
            inss.append(e.dma_start(out=self.identb[:, :], in_=self.c_ident))
            inss.append(e.dma_start(out=self.poolA[:, :, :], in_=self.c_poolA.rearrange("a p t -> p a t")))
            inss.append(e.dma_start(out=self.Btile[:, :, :].rearrange("p a i -> p (a i)"),
                                    in_=self.sg_b_s.rearrange("l g i -> (l g i)").partition_broadcast(128)))
            return inss

        self.dma(PQ, self.csem, [], [cb], cl2)
        self.op(DVE, [], [cb], lambda e: e.memset(self.onesb[:, :], 1.0))
        self.op(DVE, [], [self.stat_b], lambda e: e.memset(self.stat[:, :], 0.0))
        self.op(DVE, [], [cb], lambda e: e.memset(self.epsc[:, :], EPS))

        bk = self.nb()

        def tr(e):
            ins = None
            for t in range(3):
                ins = e.transpose(self.ps[bk][:, t * 128:(t + 1) * 128], VR[:, t * 128:(t + 1) * 128], self.identf[:, :])
            return ins

        self.op(PE, [cb, sb], [self.ps_b[bk]], tr)
        self.op(DVE, [self.ps_b[bk]], [cb], lambda e: e.tensor_copy(out=self.vecT[:, :], in_=self.ps[bk][:, 0:384]))
        bk2 = self.nb()
        self.op(PE, [cb, sb], [self.ps_b[bk2]],
                lambda e: e.transpose(self.ps[bk2][:, 0:8 * nseq], CR[0:8 * nseq, :], self.identf[0:8 * nseq, 0:8 * nseq]))
        self.op(ACT, [self.ps_b[bk2]], [cb],
                lambda e: e.activation(out=self.condT[:, :], in_=self.ps[bk2][:, 0:8 * nseq], func=AF.Silu))
        for a0 in range(0, 16, 4):
            bk3 = self.nb()

            def tr2(e, a0=a0, bk3=bk3):
                ins = None
                for q in range(4):
                    ins = e.transpose(self.ps[bk3][:, q * 128:(q + 1) * 128], WS[:, (a0 + q) * 128:(a0 + q + 1) * 128],
                                      self.identf[:, :])
                return ins

            self.op(PE, [cb, sb], [self.ps_b[bk3]], tr2)
            self.op(DVE, [self.ps_b[bk3], cb], [cb],
                    lambda e, a0=a0, bk3=bk3: e.tensor_tensor(
                        out=self.WsT[:, a0:a0 + 4, :],
                        in0=self.ps[bk3][:, :].rearrange("p (a i) -> p a i", a=4),
                        in1=self.sgmask[:, :].unsqueeze(1).to_broadcast([128, 4, 128]),
                        op=ALU.mult))
        self.op(DVE, [cb], [cb], lambda e: e.tensor_tensor(out=self.PBS[:, :], in0=self.vecT[:, V_PB:V_PB + 8],
                                                          in1=self.vecT[:, V_PS:V_PS + 8], op=ALU.mult))
        self.mod_layer(self.layers[0]) if self.layers else None

    def mod_layer(self, l):
        nseq = self.nseq
        PE, ACT, DVE = self.PE, self.ACT, self.DVE
        cb = self.const_b
        bkm = self.nb()
        first = True
        for sl in range(12):
            slot = self.load_slab([(0, 8, 512, self.w_ada[l][:, sl * 512:(sl + 1) * 512])])
            w = self.wv(slot, 0, 8, 512)

            def mm(e, sl=sl, w=w, bkm=bkm):
                ins = None
                for fc in range(4):
                    fi = sl * 4 + fc
                    for k in range(8):
                        ins = e.matmul(self.ps[bkm][:, fi * nseq:(fi + 1) * nseq], w[:, k, fc * 128:(fc + 1) * 128],
                                       self.condT[:, k * nseq:(k + 1) * nseq], start=(k == 0), stop=(k == 7))
                return ins

            self.op(PE, [self.ring_b[slot], cb], [self.ps_b[bkm]], mm)
        mv = self.MOD[:, l * 48 * nseq:(l + 1) * 48 * nseq].rearrange("p (f s) -> p f s", s=nseq)
        self.op(DVE, [self.ps_b[bkm], cb], [cb],
                lambda e, mv=mv, bkm=bkm, l=l: e.tensor_tensor(
                    out=mv, in0=self.ps[bkm][:, 0:48 * nseq].rearrange("p (f s) -> p f s", s=nseq),
                    in1=self.vecT[:, V_BADA + l * 48:V_BADA + (l + 1) * 48].unsqueeze(2).to_broadcast([128, 48, nseq]),
                    op=ALU.add))
        for (j, gbase) in ((1, V_GMIX), (4, V_GFFN)):
            v = mv[:, j * 8:(j + 1) * 8, :]
            self.op(DVE, [cb], [cb],
                    lambda e, v=v, gbase=gbase, l=l: e.scalar_tensor_tensor(
                        out=v, in0=v, scalar=1.0, op0=ALU.add,
                        in1=self.vecT[:, gbase + l * 8:gbase + (l + 1) * 8].unsqueeze(2).to_broadcast([128, 8, nseq]),
                        op1=ALU.mult))
        for j in (2, 5):
            v = mv[:, j * 8:(j + 1) * 8, :]
            self.op(DVE, [cb], [cb], lambda e, v=v: e.tensor_scalar(out=v, in0=v, scalar1=1.0, scalar2=None, op0=ALU.add))

    def unit(self, s, hf):
        row0 = s * SEQ + hf * T
        subs = []
        for l in self.layers:
            if self.do_mixer:
                subs.append(("mix", l))
            if self.do_ffn:
                subs.append(("ffn", l))
        nargs = [((l, 1, 0, s) if k == "mix" else (l, 4, 3, s)) for (k, l) in subs]
        self.next_norm = nargs[0] if subs else None
        self.load_unit(row0)
        first_unit = (s == 0 and hf == 0)
        for i, (k, l) in enumerate(subs):
            if first_unit and (i == 0 or subs[i - 1][1] != l):
                li = self.layers.index(l)
                if li + 1 < len(self.layers):
                    self.mod_layer(self.layers[li + 1])
            self.next_norm = nargs[i + 1] if i + 1 < len(subs) else None
            if k == "ffn":
                self.ffn(l, s)
            else:
                kind, j = l % 3, l // 3
                if kind == 0:
                    self.sg_mixer(l, j, s)
                elif kind == 1:
                    self.pool_mixer(l, s, hf)
                else:
                    self.ret_mixer(l, s, hf)
        self.flush_pending()
        self.store_unit(row0)

    def load_unit(self, row0):
        PE, ACT, DVE, SPQ = self.PE, self.ACT, self.DVE, self.SPQ
        xin = [self.scr_f32(0, 2048), self.scr_f32(2048, 2048)]
        xb = self.fenced(2)
        for blk in range(NB):
            i = blk % 2
            self.dma(SPQ, self.xin_sem[i], [], [xb[i]],
                     lambda e, i=i, blk=blk: [e.dma_start(out=xin[i], in_=self.x_d[row0 + blk * 128:row0 + (blk + 1) * 128, :])])
            for h in range(2):
                bk = self.nb()

                def tr(e, i=i, h=h, bk=bk):
                    ins = None
                    for q in range(4):
                        c = h * 4 + q
                        ins = e.transpose(self.ps[bk][:, q * 128:(q + 1) * 128], xin[i][:, c * 128:(c + 1) * 128], self.identf[:, :])
                    return ins

                self.op(PE, [xb[i], self.const_b], [self.ps_b[bk]], tr)
                E = ACT if h == 0 else DVE
                wr = [self.XT_b[h * 4 + q][blk // 4] for q in range(4)]
                src = self.ps[bk][:, :].rearrange("p (a t) -> p a t", a=4)
                dst = self.XT[:, h * 4:h * 4 + 4, blk * 128:(blk + 1) * 128]
                if h == 0:
                    self.op(ACT, [self.ps_b[bk]], wr, lambda e, src=src, dst=dst: e.activation(out=dst, in_=src, func=AF.Copy))
                else:
                    self.op(DVE, [self.ps_b[bk]], wr, lambda e, src=src, dst=dst: e.tensor_copy(out=dst, in_=src))
            if blk % 4 == 3 and self.next_norm is not None:
                l_, jG_, jSH_, s_ = self.next_norm
                self.norm_mod(l_, jG_, jSH_, s_, blk // 4)

    def store_unit(self, row0):
        PE, ACT, DVE, SPQ = self.PE, self.ACT, self.DVE, self.SPQ
        YF = self.scr_f32(0, 8192).rearrange("p (c t) -> p c t", c=8)
        yout = [self.scr_f32(8192, 2048), self.scr_f32(10240, 2048)]
        fb = self.fenced(3)
        yfb, yb = fb[0], fb[1:3]
        for tg in range(TG):
            if self.final:
                self.norm_mod(None, None, None, None, tg, final_dst=(YF, yfb))
            for b4 in range(4):
                blk = tg * 4 + b4
                i = blk % 2
                for h in range(2):
                    bk = self.nb()

                    def tr(e, h=h, bk=bk, b4=b4, blk=blk):
                        ins = None
                        for q in range(4):
                            c = h * 4 + q
                            if self.final:
                                src = YF[:, c, b4 * 128:(b4 + 1) * 128]
                            else:
                                src = self.XT[:, c, blk * 128:(blk + 1) * 128]
                            ins = e.transpose(self.ps[bk][:, q * 128:(q + 1) * 128], src, self.identf[:, :])
                        return ins

                    rd = [yfb] if self.final else [self.XT_b[h * 4 + q][tg] for q in range(4)]
                    self.op(PE, rd + [self.const_b], [self.ps_b[bk]], tr)
                    dst = yout[i][:, h * 512:(h + 1) * 512]
                    if h == 0:
                        self.op(ACT, [self.ps_b[bk]], [yb[i]], lambda e, dst=dst, bk=bk: e.activation(out=dst, in_=self.ps[bk][:, :], func=AF.Copy))
                    else:
                        self.op(DVE, [self.ps_b[bk]], [yb[i]], lambda e, dst=dst, bk=bk: e.tensor_copy(out=dst, in_=self.ps[bk][:, :]))
                self.dma(SPQ, self.yout_sem[i], [yb[i]], [],
                         lambda e, i=i, blk=blk: [e.dma_start(out=self.y_d[row0 + blk * 128:row0 + (blk + 1) * 128, :], in_=yout[i])])

    def norm_mod(self, l, jG, jSH, s, tg, final_dst=None):
        PE, ACT, DVE = self.PE, self.ACT, self.DVE
        sl = slice(tg * 512, (tg + 1) * 512)
        bk = self.nb()
        for c in range(8):
            tb, tbb = self.tb_next()
            self.op(ACT, [self.XT_b[c][tg]], [tbb], lambda e, tb=tb, c=c: e.activation(out=tb[:, :], in_=self.XT[:, c, sl], func=AF.Square))
            self.op(PE, [tbb, self.const_b], [self.ps_b[bk]],
                    lambda e, tb=tb, c=c: e.matmul(self.ps[bk][:, :], self.onesb[:, :], tb[:, :], start=(c == 0), stop=(c == 7)))
        rt, rtb = self.tf_next()
        self.op(ACT, [self.ps_b[bk]], [rtb], lambda e: e.activation(out=rt[:, :], in_=self.ps[bk][:, :], func=AF.Sqrt, scale=1.0 / D, bias=self.epsc[:, 0:1]))
        self.op(DVE, [rtb], [self.ps_b[bk]], lambda e: e.reciprocal(out=self.ps[bk][:, :], in_=rt[:, :]))
        for c in range(8):
            tf, tfb = self.tf_next()
            self.op(DVE, [self.XT_b[c][tg], self.ps_b[bk]], [tfb],
                    lambda e, tf=tf, c=c: e.tensor_tensor(out=tf[:, :], in0=self.XT[:, c, sl], in1=self.ps[bk][:, :], op=ALU.mult))
            if final_dst is None:
                self.op(ACT, [tfb, self.const_b], [self.HT_b[c][tg]],
                        lambda e, tf=tf, c=c: e.activation(out=self.HT[:, c, sl], in_=tf[:, :], func=AF.Identity,
                                                           scale=self.mod_ap(l, jG, c, s), bias=self.mod_ap(l, jSH, c, s)))
            else:
                YF, yfb = final_dst
                self.op(ACT, [tfb, self.const_b], [yfb],
                        lambda e, tf=tf, c=c: e.activation(out=YF[:, c, :], in_=tf[:, :], func=AF.Copy,
                                                           scale=self.vecT[:, V_GFIN + c:V_GFIN + c + 1]))

    def resid(self, bk, fo, tg, l, jGT, s):
        sl = slice(tg * 512, (tg + 1) * 512)
        self.op(self.DVE, [self.ps_b[bk], self.XT_b[fo][tg], self.const_b], [self.XT_b[fo][tg]],
                lambda e: e.scalar_tensor_tensor(out=self.XT[:, fo, sl], in0=self.ps[bk][:, :], scalar=self.mod_ap(l, jGT, fo, s),
                                                 op0=ALU.mult, in1=self.XT[:, fo, sl], op1=ALU.add))

    def proj_out(self, groups, act_ap, act_b, l, jGT, s, hook=True):
        PE = self.PE
        for gi, grp in enumerate(groups):
            last = (gi == len(groups) - 1)
            for tg in range(TG):
                for fo in range(8):
                    bk = self.nb()

                    def mm(e, grp=grp, tg=tg, fo=fo, bk=bk):
                        ins = None
                        n = len(grp)
                        for i, (wk, k, _) in enumerate(grp):
                            ins = e.matmul(self.ps[bk][:, :], wk[:, fo * 128:(fo + 1) * 128], act_ap(k, tg), start=(i == 0), stop=(i == n - 1))
                        return ins

                    rd = [act_b(k, tg) for (_, k, _) in grp] + list({id(b): b for (_, _, b) in grp}.values())
                    self.op(PE, rd, [self.ps_b[bk]], mm)
                    self.resid(bk, fo, tg, l, jGT, s)
                    if last and hook and fo == 2:
                        self.flush_pending()
                if last and hook and self.next_norm is not None:
                    self.flush_pending()
                    self.pending = (self.next_norm, tg)

    def ffn(self, l, s):
        PE, ACT, DVE = self.PE, self.ACT, self.DVE
        hb = self.fenced(22 * TG)
        HID = self.scr_bf(0, 22 * T).rearrange("p (f t) -> p f t", f=22)
        widths = [512, 512, 512, 512, 512, 256]
        c0 = 0
        for wdt in widths:
            nf = wdt // 128
            slot_a = self.load_slab([(0, 8, wdt, self.w_ffn_in[l][:, c0:c0 + wdt])])
            slot_b = self.load_slab([(0, 8, wdt, self.w_ffn_in[l][:, DFF + c0:DFF + c0 + wdt])])
            wa = self.wv(slot_a, 0, 8, wdt)
            wb = self.wv(slot_b, 0, 8, wdt)
            for tg in range(TG):
                for q in range(nf):
                    f = c0 // 128 + q
                    sl = slice(tg * 512, (tg + 1) * 512)
                    ba, bb = self.nb(), self.nb()
                    for (w, slot, bk) in ((wa, slot_a, ba), (wb, slot_b, bb)):
                        def mm(e, w=w, bk=bk, q=q, sl=sl):
                            ins = None
                            for k in range(8):
                                ins = e.matmul(self.ps[bk][:, :], w[:, k, q * 128:(q + 1) * 128], self.HT[:, k, sl], start=(k == 0), stop=(k == 7))
                            return ins
                        self.op(PE, [self.ring_b[slot]] + self.htb(tg), [self.ps_b[bk]], mm)
                    self.flush_pending()
                    tf, tfb = self.tf_next()
                    self.op(ACT, [self.ps_b[ba]], [tfb], lambda e, tf=tf, ba=ba: e.activation(out=tf[:, :], in_=self.ps[ba][:, :], func=AF.Silu))
                    self.op(DVE, [tfb, self.ps_b[bb]], [hb[f * TG + tg]],
                            lambda e, tf=tf, bb=bb, f=f, sl=sl: e.tensor_tensor(out=HID[:, f, sl], in0=tf[:, :], in1=self.ps[bb][:, :], op=ALU.mult))
            c0 += wdt
        groups = []
        for (k0, nks) in ((0, [4, 4, 4]), (12, [4, 4, 2])):
            grp = []
            k = k0
            for nk in nks:
                slot = self.load_slab([(0, nk, 1024, self.w_ffn_out[l][k * 128:(k + nk) * 128, :])])
                w = self.wv(slot, 0, nk, 1024)
                for i in range(nk):
                    grp.append((w[:, i, :], k + i, self.ring_b[slot]))
                k += nk
            groups.append(grp)
        self.proj_out(groups, lambda k, tg: HID[:, k, tg * 512:(tg + 1) * 512], lambda k, tg: hb[k * TG + tg], l, 5, s)

    def sg_mixer(self, l, j, s):
        PE, ACT, DVE = self.PE, self.ACT, self.DVE
        fb = self.fenced(4 + 16 * TG)
        vb = fb[0:4]
        mxb = fb[4:]
        VB = self.scr_bf(0, 8192).rearrange("p (b f) -> p b f", b=4)
        MX = self.scr_bf(8192, 16 * T).rearrange("p (c t) -> p c t", c=16)
        vs = []
        for i in range(4):
            vs.append(self.load_slab([(0, 8, 512, self.sg_w_in[j][:, 2048 + i * 512:2048 + (i + 1) * 512])]))
        stat = self.stat
        sbq = self.fenced(2)

        deferred = []

        def v_phase(qi):
            p = qi % 2
            base = p * 24
            for b2 in range(2):
                blk = qi * 2 + b2
                slot4 = blk % 4
                tgb = blk // 4
                for fs in range(4):
                    bk = self.nb()
                    w = self.wv(vs[fs], 0, 8, 512)

                    def mm(e, w=w, bk=bk, blk=blk):
                        ins = None
                        for k in range(8):
                            ins = e.matmul(self.ps[bk][:, :], self.HT[:, k, blk * 128:(blk + 1) * 128], w[:, k, :], start=(k == 0), stop=(k == 7))
                        return ins

                    self.op(PE, [self.ring_b[vs[fs]]] + self.htb(tgb), [self.ps_b[bk]], mm)
                    self.flush_pending()
                    c1 = base + b2 * 4 + fs
                    self.op(ACT, [self.ps_b[bk]], [vb[slot4], sbq[p]],
                            lambda e, bk=bk, slot4=slot4, fs=fs, c1=c1: e.activation(out=VB[:, slot4, fs * 512:(fs + 1) * 512], in_=self.ps[bk][:, :], func=AF.Gelu,
                                                                                    accum_out=stat[:, c1:c1 + 1]))
                    if deferred:
                        deferred.pop()()

                    def sq(slot4=slot4, fs=fs, c1=c1, p=p):
                        tb, tbb = self.tb_next()
                        self.op(ACT, [vb[slot4]], [tbb, sbq[p]],
                                lambda e, tb=tb: e.activation(out=tb[:, :], in_=VB[:, slot4, fs * 512:(fs + 1) * 512], func=AF.Square,
                                                              accum_out=stat[:, c1 + 8:c1 + 9]))
                    deferred.append(sq)
            if deferred:
                deferred.pop()()

        def stat_phase(qi):
            p = qi % 2
            base = p * 24
            sb = sbq[p]
            X = mybir.AxisListType.X
            self.op(DVE, [sb], [sb], lambda e: e.tensor_reduce(out=stat[:, base + 16:base + 18], in_=stat[:, base:base + 8].rearrange("p (b f) -> p b f", f=4), axis=X, op=ALU.add))
            self.op(DVE, [sb], [sb], lambda e: e.tensor_reduce(out=stat[:, base + 18:base + 20], in_=stat[:, base + 8:base + 16].rearrange("p (b f) -> p b f", f=4), axis=X, op=ALU.add))
            self.op(DVE, [sb], [sb], lambda e: e.tensor_scalar(out=stat[:, base + 16:base + 20], in0=stat[:, base + 16:base + 20], scalar1=1.0 / 2048, scalar2=None, op0=ALU.mult))
            self.op(DVE, [sb], [sb], lambda e: e.tensor_tensor(out=stat[:, base + 22:base + 24], in0=stat[:, base + 16:base + 18], in1=stat[:, base + 16:base + 18], op=ALU.mult))
            self.op(DVE, [sb], [sb], lambda e: e.tensor_tensor(out=stat[:, base + 18:base + 20], in0=stat[:, base + 18:base + 20], in1=stat[:, base + 22:base + 24], op=ALU.subtract))
            self.op(ACT, [sb], [sb], lambda e: e.activation(out=stat[:, base + 22:base + 24], in_=stat[:, base + 18:base + 20], func=AF.Sqrt, bias=self.epsc[:, 0:1]))
            self.op(DVE, [sb], [sb], lambda e: e.reciprocal(out=stat[:, base + 20:base + 22], in_=stat[:, base + 22:base + 24]))

        def mix_phase(qi):
            p = qi % 2
            base = p * 24
            for b2 in range(2):
                blk = qi * 2 + b2
                slot4 = blk % 4
                tgb = blk // 4
                self.op(DVE, [vb[slot4], sbq[p]], [vb[slot4]],
                        lambda e, slot4=slot4, b2=b2: e.tensor_scalar(out=VB[:, slot4, :], in0=VB[:, slot4, :], scalar1=stat[:, base + 16 + b2:base + 17 + b2],
                                                                     scalar2=stat[:, base + 20 + b2:base + 21 + b2], op0=ALU.subtract, op1=ALU.mult))
                for c4 in range(4):
                    bk = self.nb()

                    def mm2(e, bk=bk, slot4=slot4, c4=c4):
                        ins = None
                        for q in range(4):
                            cc = c4 * 4 + q
                            g = cc // 2
                            ins = e.matmul(self.ps[bk][:, q * 128:(q + 1) * 128], VB[:, slot4, cc * 128:(cc + 1) * 128], self.WsT[:, j * 8 + g, :], start=True, stop=True)
                        return ins

                    self.op(PE, [vb[slot4], self.const_b], [self.ps_b[bk]], mm2)
                    for q in range(4):
                        cc = c4 * 4 + q
                        g = cc // 2
                        self.op(DVE, [self.ps_b[bk], self.const_b], [mxb[cc * TG + tgb]],
                                lambda e, bk=bk, q=q, cc=cc, g=g, blk=blk: e.scalar_tensor_tensor(
                                    out=MX[:, cc, blk * 128:(blk + 1) * 128], in0=self.ps[bk][:, q * 128:(q + 1) * 128],
                                    scalar=self.vecT[:, V_SGG + j * 16 + cc:V_SGG + j * 16 + cc + 1], op0=ALU.mult,
                                    in1=self.Btile[:, j * 8 + g, :], op1=ALU.add))

        NQ = NB // 2
        v_phase(0)
        for qi in range(NQ):
            stat_phase(qi)
            if qi + 1 < NQ:
                v_phase(qi + 1)
            mix_phase(qi)
        for i in range(4):
            slot = self.load_slab([(0, 8, 512, self.sg_w_in[j][:, i * 512:(i + 1) * 512])])
            w = self.wv(slot, 0, 8, 512)
            for q in range(4):
                cc = i * 4 + q
                for tg in range(TG):
                    sl = slice(tg * 512, (tg + 1) * 512)
                    bk = self.nb()

                    def mm(e, w=w, bk=bk, q=q, sl=sl):
                        ins = None
                        for k in range(8):
                            ins = e.matmul(self.ps[bk][:, :], w[:, k, q * 128:(q + 1) * 128], self.HT[:, k, sl], start=(k == 0), stop=(k == 7))
                        return ins

                    self.op(PE, [self.ring_b[slot]] + self.htb(tg), [self.ps_b[bk]], mm)
                    tb, tbb = self.tb_next()
                    self.op(ACT, [self.ps_b[bk]], [tbb], lambda e, tb=tb, bk=bk: e.activation(out=tb[:, :], in_=self.ps[bk][:, :], func=AF.Gelu))
                    self.op(DVE, [tbb, mxb[cc * TG + tg]], [mxb[cc * TG + tg]],
                            lambda e, tb=tb, cc=cc, sl=sl: e.tensor_tensor(out=MX[:, cc, sl], in0=MX[:, cc, sl], in1=tb[:, :], op=ALU.mult))
        grp = []
        for i in range(4):
            slot = self.load_slab([(0, 4, 1024, self.sg_w_out[j][i * 512:(i + 1) * 512, :])])
            w = self.wv(slot, 0, 4, 1024)
            for q in range(4):
                grp.append((w[:, q, :], i * 4 + q, self.ring_b[slot]))
        self.proj_out([grp], lambda k, tg: MX[:, k, tg * 512:(tg + 1) * 512], lambda k, tg: mxb[k * TG + tg], l, 2, s)

    def pool_mixer(self, l, s, hf):
        PE, ACT, DVE = self.PE, self.ACT, self.DVE
        fb = self.fenced(3 + 8 * TG + 8 * TG)
        zb = fb[0:3]
        ptb = fb[3:3 + 8 * TG]
        y2b = fb[3 + 8 * TG:]
        ZB = self.scr_bf(0, 3072).rearrange("p (i f) -> p i f", i=3)
        PT = self.scr_bf(3072, 8 * T).rearrange("p (c t) -> p c t", c=8)
        Y2 = self.scr_bf(3072 + 8 * T, 8 * T).rearrange("p (c t) -> p c t", c=8)
        ws = [self.load_slab([(0, 8, 512, self.pool_w_in[:, i * 512:(i + 1) * 512])]) for i in range(2)]

        def z_phase(blk):
            tgb = blk // 4
            cur = blk % 3
            for fs in range(2):
                bk = self.nb()
                w = self.wv(ws[fs], 0, 8, 512)

                def mm(e, w=w, bk=bk, blk=blk):
                    ins = None
                    for k in range(8):
                        ins = e.matmul(self.ps[bk][:, :], self.HT[:, k, blk * 128:(blk + 1) * 128], w[:, k, :], start=(k == 0), stop=(k == 7))
                    return ins

                self.op(PE, [self.ring_b[ws[fs]]] + self.htb(tgb), [self.ps_b[bk]], mm)
                self.flush_pending()
                if fs == 0:
                    self.op(ACT, [self.ps_b[bk]], [zb[cur]], lambda e, bk=bk, cur=cur: e.activation(out=ZB[:, cur, 0:512], in_=self.ps[bk][:, :], func=AF.Copy))
                else:
                    self.op(DVE, [self.ps_b[bk]], [zb[cur]], lambda e, bk=bk, cur=cur: e.tensor_copy(out=ZB[:, cur, 512:1024], in_=self.ps[bk][:, :]))

        def p_phase(blk):
            tgb = blk // 4
            cur = blk % 3
            prv = (blk - 1) % 3
            first = (hf == 0 and blk == 0)
            if blk == 0:
                prev_ap = lambda cc: self.ZC[:, cc * 128:(cc + 1) * 128]
                prev_b = self.ZC_b
            else:
                prev_ap = lambda cc, prv=prv: ZB[:, prv, cc * 128:(cc + 1) * 128]
                prev_b = zb[prv]
            for h2 in range(2):
                bk = self.nb()

                def mm3(e, bk=bk, h2=h2, cur=cur, first=first, prev_ap=prev_ap):
                    ins = None
                    for q in range(4):
                        cc = h2 * 4 + q
                        g = cc // 2
                        o = self.ps[bk][:, q * 128:(q + 1) * 128]
                        if first:
                            ins = e.matmul(o, ZB[:, cur, cc * 128:(cc + 1) * 128], self.poolA[:, 8 + g, :], start=True, stop=True)
                        else:
                            e.matmul(o, ZB[:, cur, cc * 128:(cc + 1) * 128], self.poolA[:, g, :], start=True, stop=False)
                            ins = e.matmul(o, prev_ap(cc), self.poolA[:, 4 + g, :], start=False, stop=True)
                    return ins

                rd = [zb[cur], self.const_b] + ([] if first else [prev_b])
                self.op(PE, rd, [self.ps_b[bk]], mm3)
                wr = [ptb[(h2 * 4 + q) * TG + tgb] for q in range(4)]
                dst = PT[:, h2 * 4:h2 * 4 + 4, blk * 128:(blk + 1) * 128]
                src = self.ps[bk][:, :].rearrange("p (a t) -> p a t", a=4)
                if h2 == 0:
                    self.op(ACT, [self.ps_b[bk]], wr, lambda e, dst=dst, src=src: e.activation(out=dst, in_=src, func=AF.Copy))
                else:
                    self.op(DVE, [self.ps_b[bk]], wr, lambda e, dst=dst, src=src: e.tensor_copy(out=dst, in_=src))
            if blk == NB - 1 and hf < UPS - 1:
                self.op(DVE, [zb[cur]], [self.ZC_b], lambda e, cur=cur: e.tensor_copy(out=self.ZC[:, :], in_=ZB[:, cur, :]))

        z_phase(0)
        for blk in range(NB):
            if blk + 1 < NB:
                z_phase(blk + 1)
            p_phase(blk)
        slot = self.load_slab([(0, 8, 256, self.pool_w_grp.rearrange("g c d -> (g c) d"))])
        wg = self.wv(slot, 0, 8, 256)
        for dd in range(8):
            g, dc = dd // 2, dd % 2
            for tg in range(TG):
                sl = slice(tg * 512, (tg + 1) * 512)
                bk = self.nb()

                def mm(e, bk=bk, g=g, dc=dc, sl=sl):
                    e.matmul(self.ps[bk][:, :], wg[:, g * 2, dc * 128:(dc + 1) * 128], PT[:, g * 2, sl], start=True, stop=False)
                    return e.matmul(self.ps[bk][:, :], wg[:, g * 2 + 1, dc * 128:(dc + 1) * 128], PT[:, g * 2 + 1, sl], start=False, stop=True)

                self.op(PE, [self.ring_b[slot], ptb[(g * 2) * TG + tg], ptb[(g * 2 + 1) * TG + tg]], [self.ps_b[bk]], mm)
                self.op(ACT, [self.ps_b[bk], self.const_b], [y2b[dd * TG + tg]],
                        lambda e, bk=bk, dd=dd, sl=sl: e.activation(out=Y2[:, dd, sl], in_=self.ps[bk][:, :], func=AF.Identity,
                                                                    scale=self.vecT[:, V_PS + dd:V_PS + dd + 1], bias=self.PBS[:, dd:dd + 1]))
        grp = []
        for i in range(2):
            slot = self.load_slab([(0, 4, 1024, self.pool_w_out[i * 512:(i + 1) * 512, :])])
            w = self.wv(slot, 0, 4, 1024)
            for q in range(4):
                grp.append((w[:, q, :], i * 4 + q, self.ring_b[slot]))
        self.proj_out([grp], lambda k, tg: Y2[:, k, tg * 512:(tg + 1) * 512], lambda k, tg: y2b[k * TG + tg], l, 2, s)

    def ret_mixer(self, l, s, hf):
        PE, ACT, DVE, SPQ = self.PE, self.ACT, self.DVE, self.SPQ
        nfb = 2 * TG + 2 * TG + NB + NB + 4 * TG + 1
        fb = self.fenced(nfb)
        qb = fb[0:2 * TG]
        kb = fb[2 * TG:4 * TG]
        vb = fb[4 * TG:4 * TG + NB]
        kdb = fb[4 * TG + NB:4 * TG + 2 * NB]
        gsb = fb[4 * TG + 2 * NB:4 * TG + 2 * NB + 4 * TG]
        obb = vb
        rb = fb[-1]
        o = 0
        QT = self.scr_bf(o, 2 * T).rearrange("p (c t) -> p c t", c=2); o += 2 * T
        KT = self.scr_bf(o, 2 * T).rearrange("p (c t) -> p c t", c=2); o += 2 * T
        V = self.scr_bf(o, NB * 512).rearrange("p (b e) -> p b e", b=NB); o += NB * 512
        OB = V
        KD = self.scr_bf(o, NB * 256).rearrange("p (b d) -> p b d", b=NB); o += NB * 256
        GS = self.scr_bf(o, 4 * T).rearrange("p (c t) -> p c t", c=4); o += 4 * T
        ROPE = self.scr_f32(o, 4 * T).rearrange("p (a t) -> p a t", a=2)
        o += 4 * T
        assert o <= SCR
        self.dma(SPQ, self.rope_sem, [], [rb],
                 lambda e: [e.dma_start(out=ROPE, in_=self.c_rope[:, :, hf * T:(hf + 1) * T].rearrange("a p t -> p a t"))])
        stat = self.stat
        sb = self.stat_b
        if hf == 0:
            for i in range(8):
                self.op(DVE, [], [self.S_b[i]], lambda e, i=i: e.memset(self.S32[:, i, :], 0.0))
                self.op(DVE, [], [self.SBF_b[i]], lambda e, i=i: e.memset(self.SBF[:, i, :], 0.0))
        Win = self.ret_w_in
        for h in range(4):
            gam = 1.0 - 2.0 ** (-5 - h)
            slot = self.load_slab([(0, 8, 256, Win[:, h * 256:(h + 1) * 256]), (2048, 8, 256, Win[:, 1024 + h * 256:1024 + (h + 1) * 256])])
            for tg in range(TG):
                for (which, off, DST, dbufs) in ((0, 0, QT, qb), (1, 2048, KT, kb)):
                    w = self.wv(slot, off, 8, 256)
                    sl = slice(tg * 512, (tg + 1) * 512)
                    b1, b2 = self.nb(), self.nb()
                    for (dc, bk) in ((0, b1), (1, b2)):
                        def mm(e, w=w, bk=bk, dc=dc, sl=sl):
                            ins = None
                            for k in range(8):
                                ins = e.matmul(self.ps[bk][:, :], w[:, k, dc * 128:(dc + 1) * 128], self.HT[:, k, sl], start=(k == 0), stop=(k == 7))
                            return ins
                        self.op(PE, [self.ring_b[slot]] + self.htb(tg), [self.ps_b[bk]], mm)
                    self.flush_pending()
                    cosv = ROPE[:, 0, sl]
                    sinv = ROPE[:, 1, sl]
                    t1, t1b = self.tf_next()
                    t2, t2b = self.tf_next()
                    self.op(DVE, [self.ps_b[b1], rb], [t1b], lambda e, t1=t1, b1=b1, cosv=cosv: e.tensor_tensor(out=t1[:, :], in0=self.ps[b1][:, :], in1=cosv, op=ALU.mult))
                    self.op(DVE, [self.ps_b[b2], rb], [t2b], lambda e, t2=t2, b2=b2, sinv=sinv: e.tensor_tensor(out=t2[:, :], in0=self.ps[b2][:, :], in1=sinv, op=ALU.mult))
                    self.op(DVE, [t1b, t2b], [dbufs[0 * TG + tg]], lambda e, t1=t1, t2=t2, DST=DST, sl=sl: e.tensor_tensor(out=DST[:, 0, sl], in0=t1[:, :], in1=t2[:, :], op=ALU.subtract))
                    t3, t3b = self.tf_next()
                    t4, t4b = self.tf_next()
                    self.op(DVE, [self.ps_b[b2], rb], [t3b], lambda e, t3=t3, b2=b2, cosv=cosv: e.tensor_tensor(out=t3[:, :], in0=self.ps[b2][:, :], in1=cosv, op=ALU.mult))
                    self.op(DVE, [self.ps_b[b1], rb], [t4b], lambda e, t4=t4, b1=b1, sinv=sinv: e.tensor_tensor(out=t4[:, :], in0=self.ps[b1][:, :], in1=sinv, op=ALU.mult))
                    self.op(DVE, [t3b, t4b], [dbufs[1 * TG + tg]], lambda e, t3=t3, t4=t4, DST=DST, sl=sl: e.tensor_tensor(out=DST[:, 1, sl], in0=t3[:, :], in1=t4[:, :], op=ALU.add))
            slot_v = self.load_slab([(0, 8, 512, Win[:, 2048 + h * 512:2048 + (h + 1) * 512])])
            wvv = self.wv(slot_v, 0, 8, 512)
            for blk in range(NB):
                bk = self.nb()

                def mm(e, bk=bk, blk=blk):
                    ins = None
                    for k in range(8):
                        ins = e.matmul(self.ps[bk][:, :], self.HT[:, k, blk * 128:(blk + 1) * 128], wvv[:, k, :], start=(k == 0), stop=(k == 7))
                    return ins

                self.op(PE, [self.ring_b[slot_v]] + self.htb(blk // 4), [self.ps_b[bk]], mm)
                self.op(ACT, [self.ps_b[bk]], [vb[blk]], lambda e, bk=bk, blk=blk: e.activation(out=V[:, blk, :], in_=self.ps[bk][:, :], func=AF.Copy))
            for blk in range(NB):
                bk = self.nb()
                pbf = self.ps[bk][:, :].bitcast(BF16)

                def tr(e, blk=blk, pbf=pbf):
                    e.transpose(pbf[:, 0:128], KT[:, 0, blk * 128:(blk + 1) * 128], self.identb[:, :])
                    return e.transpose(pbf[:, 128:256], KT[:, 1, blk * 128:(blk + 1) * 128], self.identb[:, :])

                self.op(PE, [kb[0 * TG + blk // 4], kb[1 * TG + blk // 4], self.const_b], [self.ps_b[bk]], tr)
                self.op(ACT, [self.ps_b[bk], self.const_b], [kdb[blk]],
                        lambda e, blk=blk, pbf=pbf, h=h: e.activation(out=KD[:, blk, :], in_=pbf[:, 0:256], func=AF.Copy, scale=self.rvec[:, h:h + 1]))
            slot_g = self.load_slab([(0, 8, 512, Win[:, 4096 + h * 512:4096 + (h + 1) * 512])])
            wg = self.wv(slot_g, 0, 8, 512)
            g_jobs = [(gc, tg) for tg in range(TG) for gc in range(4)]

            def g_group(gc, tg, wg=wg, slot_g=slot_g):
                sl = slice(tg * 512, (tg + 1) * 512)
                bk = self.nb()

                def mm(e, bk=bk, gc=gc, sl=sl):
                    ins = None
                    for k in range(8):
                        ins = e.matmul(self.ps[bk][:, :], wg[:, k, gc * 128:(gc + 1) * 128], self.HT[:, k, sl], start=(k == 0), stop=(k == 7))
                    return ins

                self.op(PE, [self.ring_b[slot_g]] + self.htb(tg), [self.ps_b[bk]], mm)
                self.op(ACT, [self.ps_b[bk]], [gsb[gc * TG + tg]], lambda e, bk=bk, gc=gc, sl=sl: e.activation(out=GS[:, gc, sl], in_=self.ps[bk][:, :], func=AF.Silu))

            for blk in range(NB):
                tgb = blk // 4
                bsl = slice(blk * 128, (blk + 1) * 128)
                bks = self.nb()

                def mms(e, bks=bks, bsl=bsl):
                    e.matmul(self.ps[bks][:, 0:128], KT[:, 0, bsl], QT[:, 0, bsl], start=True, stop=False)
                    return e.matmul(self.ps[bks][:, 0:128], KT[:, 1, bsl], QT[:, 1, bsl], start=False, stop=True)

                self.op(PE, [kb[tgb], kb[TG + tgb], qb[tgb], qb[TG + tgb]], [self.ps_b[bks]], mms)
                pm, pmb = self.tb_next()
                self.op(DVE, [self.ps_b[bks], self.const_b], [pmb],
                        lambda e, pm=pm, bks=bks, h=h: e.tensor_tensor(out=pm[:, 0:128], in0=self.ps[bks][:, 0:128], in1=self.dmask[:, h, :], op=ALU.mult))
                for _ in range((len(g_jobs) + (NB - blk) - 1) // (NB - blk)):
                    g_group(*g_jobs.pop(0))
                bko = self.nb()

                def mmo(e, bko=bko, pm=pm, blk=blk, bsl=bsl, h=h):
                    e.matmul(self.ps[bko][:, :], pm[:, 0:128], V[:, blk, :], start=True, stop=False)
                    e.matmul(self.ps[bko][:, :], QT[:, 0, bsl], self.SBF[:, h * 2, :], start=False, stop=False)
                    return e.matmul(self.ps[bko][:, :], QT[:, 1, bsl], self.SBF[:, h * 2 + 1, :], start=False, stop=True)

                self.op(PE, [pmb, vb[blk], qb[tgb], qb[TG + tgb], self.SBF_b[h * 2], self.SBF_b[h * 2 + 1]], [self.ps_b[bko]], mmo)
                for dc in range(2):
                    bkd = self.nb()
                    self.op(PE, [kdb[blk], vb[blk]], [self.ps_b[bkd]],
                            lambda e, bkd=bkd, blk=blk, dc=dc: e.matmul(self.ps[bkd][:, :], KD[:, blk, dc * 128:(dc + 1) * 128], V[:, blk, :], start=True, stop=True))
                    i = h * 2 + dc
                    self.op(DVE, [self.ps_b[bkd], self.S_b[i]], [self.S_b[i]],
                            lambda e, bkd=bkd, i=i, gam=gam: e.scalar_tensor_tensor(out=self.S32[:, i, :], in0=self.S32[:, i, :], scalar=float(gam ** 128), op0=ALU.mult,
                                                                                  in1=self.ps[bkd][:, :], op1=ALU.add))
                    self.op(ACT, [self.S_b[i]], [self.SBF_b[i]], lambda e, i=i: e.activation(out=self.SBF[:, i, :], in_=self.S32[:, i, :], func=AF.Copy))
                tj, tjb = self.tf_next()
                self.op(ACT, [self.ps_b[bko]], [tjb, sb],
                        lambda e, tj=tj, bko=bko, blk=blk: e.activation(out=tj[:, :], in_=self.ps[bko][:, :], func=AF.Square, accum_out=stat[:, 48 + blk:49 + blk]))
                self.op(DVE, [self.ps_b[bko]], [obb[blk]], lambda e, bko=bko, blk=blk: e.tensor_copy(out=OB[:, blk, :], in_=self.ps[bko][:, :]))
            while g_jobs:
                g_group(*g_jobs.pop(0))
            self.op(ACT, [sb, self.const_b], [sb], lambda e, h=h: e.activation(out=stat[:, 56:56 + NB], in_=stat[:, 48:48 + NB], func=AF.Sqrt, scale=1.0 / 512, bias=self.rvec[:, 4 + h:5 + h]))
            self.op(DVE, [sb], [sb], lambda e: e.reciprocal(out=stat[:, 48:48 + NB], in_=stat[:, 56:56 + NB]))
            for blk in range(NB):
                self.op(DVE, [obb[blk], sb], [obb[blk]],
                        lambda e, blk=blk: e.tensor_scalar(out=OB[:, blk, :], in0=OB[:, blk, :], scalar1=stat[:, 48 + blk:49 + blk], scalar2=None, op0=ALU.mult))
            for blk in range(NB):
                tgb = blk // 4
                bk = self.nb()
                pbf = self.ps[bk][:, :].bitcast(BF16)

                def tr(e, blk=blk, pbf=pbf):
                    ins = None
                    for ec in range(4):
                        ins = e.transpose(pbf[:, ec * 128:(ec + 1) * 128], OB[:, blk, ec * 128:(ec + 1) * 128], self.identb[:, :])
                    return ins

                self.op(PE, [obb[blk], self.const_b], [self.ps_b[bk]], tr)
                wr = [gsb[ec * TG + tgb] for ec in range(4)]
                dst = GS[:, :, blk * 128:(blk + 1) * 128]
                self.op(DVE, [self.ps_b[bk]] + wr, wr,
                        lambda e, dst=dst, pbf=pbf: e.tensor_tensor(out=dst, in0=pbf[:, 0:512].rearrange("p (a t) -> p a t", a=4), in1=dst, op=ALU.mult))
            slot_o = self.load_slab([(0, 4, 1024, self.ret_w_out[h * 512:(h + 1) * 512, :])])
            wo = self.wv(slot_o, 0, 4, 1024)
            grp = [(wo[:, q, :], q, self.ring_b[slot_o]) for q in range(4)]
            self.proj_out([grp], lambda k, tg: GS[:, k, tg * 512:(tg + 1) * 512], lambda k, tg: gsb[k * TG + tg], l, 2, s, hook=(h == 3))


def _consts():
    ident = np.eye(128, dtype=np.float32)
    A = np.zeros((12, 128, 128), np.float32)
    tp = np.arange(128)[:, None]
    t = np.arange(128)[None, :]
    for g, win in enumerate((2, 4, 8, 16)):
        d = t - tp
        A[g] = ((d >= 0) & (d < win)).astype(np.float32) / win - (d == 0).astype(np.float32)
        dp = t - (tp - 128)
        A[4 + g] = ((dp >= 0) & (dp < win)).astype(np.float32) / win
        cnt = np.minimum(t + 1, win).astype(np.float32)
        A[8 + g] = ((d >= 0) & (d < win)).astype(np.float32) / cnt - (d == 0).astype(np.float32)
    inv = (10000.0 ** (-np.arange(0, 256, 2, dtype=np.float32) / np.float32(256))).astype(np.float32)
    pos = np.arange(SEQ, dtype=np.float32)
    ang = (pos[:, None] * inv[None, :]).astype(np.float32)
    rope = np.stack([np.cos(ang).T, np.sin(ang).T]).astype(np.float32)
    dm = np.zeros((4, 128, 128), np.float64)
    rvec = np.zeros((128, 8), np.float64)
    idx = np.arange(128, dtype=np.float64)
    for h in range(4):
        gam = 1.0 - 2.0 ** (-5 - h)
        lg = np.log(gam)
        c = idx[None, :]
        m = idx[:, None]
        same = (np.floor(c / 64) == np.floor(m / 64))
        past = (np.floor(m / 64) < np.floor(c / 64))
        Dm = np.where(same, np.exp(lg * np.abs(c - m)), np.where(past, np.exp(lg * (c - m)), 0.0))
        qd = np.exp(lg * (idx + 1.0))
        dm[h] = Dm / qd[None, :] / 16.0
        rvec[:, h] = np.exp(lg * (127.0 - idx)) / 16.0
        rvec[:, 4 + h] = EPS / qd ** 2
    jj = np.arange(128)[:, None]
    ii = np.arange(128)[None, :]
    sgm = ((ii // 64) >= (jj // 64)).astype(np.float32)
    return dict(c_ident=ident, c_poolA=A, c_rope=np.ascontiguousarray(rope), c_dmask=dm.astype(np.float32),
                c_rvec=rvec.astype(np.float32), c_sgmask=sgm)


def _pack_vecs(inp):
    v = np.zeros((V_ROWS, 128), np.float32)
    v[V_GMIX:V_GMIX + 32] = np.asarray(inp["norm_mix_g"], np.float32).reshape(32, 128)
    v[V_GFFN:V_GFFN + 32] = np.asarray(inp["norm_ffn_g"], np.float32).reshape(32, 128)
    v[V_BADA:V_BADA + 192] = np.asarray(inp["b_ada"], np.float32).reshape(192, 128)
    v[V_GFIN:V_GFIN + 8] = np.asarray(inp["final_norm_g"], np.float32).reshape(8, 128)
    v[V_SGG:V_SGG + 32] = np.asarray(inp["sg_v_norm_g"], np.float32).reshape(32, 128)
    v[V_PB:V_PB + 8] = np.asarray(inp["pool_b_grp"], np.float32).reshape(8, 128)
    v[V_PS:V_PS + 8] = np.asarray(inp["pool_scale"], np.float32).reshape(8, 128)
    return v


_NC_CACHE = {}


def _weights(inp, layers, do_mixer, do_ffn):
    f = lambda a: np.ascontiguousarray(np.asarray(a, np.float32))
    w = {}
    for l in layers:
        w[f"w_ada{l}"] = f(inp["w_ada"][l])
        if do_ffn:
            w[f"w_ffn_in{l}"] = f(inp["w_ffn_in"][l])
            w[f"w_ffn_out{l}"] = f(inp["w_ffn_out"][l])
        if do_mixer and l % 3 == 0:
            w[f"sg_w_in{l // 3}"] = f(inp["sg_w_in"][l // 3])
            w[f"sg_w_out{l // 3}"] = f(inp["sg_w_out"][l // 3])
        if do_mixer and l == 1:
            w["pool_w_in"] = f(inp["pool_w_in"][0])
            w["pool_w_grp"] = f(inp["pool_w_grp"][0])
            w["pool_w_out"] = f(inp["pool_w_out"][0])
        if do_mixer and l == 2:
            w["ret_w_in"] = f(inp["ret_w_in"][0])
            w["ret_w_out"] = f(inp["ret_w_out"][0])
    w["sg_w_s"] = f(inp["sg_w_s"])
    w["sg_b_s"] = f(inp["sg_b_s"])
    return w


def run_layers(inp, x, layers, final, do_mixer=True, do_ffn=True, trace=False, ncores=NCORES):
    B = x.shape[0]
    nseq = B // ncores
    key = (nseq, tuple(layers), final, do_mixer, do_ffn)
    if key not in _NC_CACHE:
        _NC_CACHE[key] = Gen(nseq, list(layers), final, do_mixer, do_ffn).build()
    nc = _NC_CACHE[key]
    consts = _consts()
    vecs = _pack_vecs(inp)
    shared = _weights(inp, layers, do_mixer, do_ffn)
    c = np.asarray(inp["c"], np.float32)
    in_maps = []
    for core in range(ncores):
        xs = np.ascontiguousarray(x[core * nseq:(core + 1) * nseq].reshape(nseq * SEQ, D))
        cs = c[core * nseq:(core + 1) * nseq]
        ct = np.ascontiguousarray(cs.reshape(nseq, 8, 128).transpose(1, 0, 2).reshape(8 * nseq, 128))
        m = {"x": xs, "ct": ct, "vecs": vecs}
        m.update(shared)
        m.update(consts)
        in_maps.append(m)
    res = run_bass_kernel_spmd(nc, in_maps, core_ids=list(range(ncores)), trace=trace)
    y = np.concatenate([r["y"].reshape(nseq, SEQ, D) for r in res.results], axis=0)
    return y, res


def kernel(**inputs):
    x = np.asarray(inputs["x"], np.float32)
    y, _ = run_layers(inputs, x, [0, 1, 2, 3], True)
    return y.astype(np.float32)
```
